# Optimizing a Trainium2 kernel written in Bass

```python
import jax, jax.numpy as jnp
from jax import lax
import numpy as np

D_MODEL = 2048
BATCH = 4
SEQ = 2048
DEPTH = 1
DEC_BATCH = 32
DEC_SEQ = 4
PAST_LEN = 8192
PAGE_SIZE = 128

SB_HEADS = 16
SB_HEAD_DIM = D_MODEL // 32
SB_WIDTH = SB_HEADS * SB_HEAD_DIM
SB_QBLOCK = 128
SB_BIAS_INIT = -7.0
GLA_HEADS = 4
GLA_V_WIDTH = D_MODEL // 2
GLA_K_WIDTH = GLA_V_WIDTH // 2
GLA_DK = GLA_K_WIDTH // GLA_HEADS
GLA_DV = GLA_V_WIDTH // GLA_HEADS
GLA_GATE_RANK = 16
GLA_TAU = 16.0
GLA_CHUNK = 64
D_FF = ((8 * D_MODEL // 3 + 255) // 256) * 256
CONV_W = 3
N_BRANCH = 2
EPS = 1e-6
IN_SIZES = (SB_WIDTH, SB_WIDTH, SB_WIDTH, GLA_K_WIDTH, GLA_K_WIDTH, GLA_V_WIDTH, GLA_V_WIDTH, GLA_GATE_RANK, D_MODEL, D_MODEL)
N_IN = sum(IN_SIZES)

kernel_name = 'hybrid_stickbreak_gla_convffn_adaln_step'


def _rmsnorm(x, g):
    xf = x.astype(jnp.float32)
    y = xf * lax.rsqrt(jnp.mean(xf * xf, axis=-1, keepdims=True) + EPS)
    return (y * g.astype(jnp.float32)).astype(x.dtype)


def _split_last(a, sizes):
    parts, off = [], 0
    for s in sizes:
        parts.append(a[..., off:off + s])
        off += s
    return parts


def _sb_block(q, k, v, t_pos, bias):
    z = jnp.einsum('bqhd,bkhd->bhqk', q, k) * (SB_HEAD_DIM ** -0.5) + bias[None, :, None, None]
    s_pos = jnp.arange(k.shape[1])
    mask = (s_pos[None, :] < t_pos[:, None])[None, None]
    log_beta = jax.nn.log_sigmoid(z)
    log_rest = jnp.where(mask, log_beta - z, 0.0)
    after = lax.cumsum(log_rest, axis=3, reverse=True) - log_rest
    a = jnp.where(mask, jnp.exp(log_beta + after), 0.0)
    return jnp.einsum('bhqk,bkhd->bqhd', a, v)


def _sb_attention(q, k, v, bias):
    q, k, v = q.astype(jnp.float32), k.astype(jnp.float32), v.astype(jnp.float32)
    bias = bias.astype(jnp.float32)
    b, tq, h, d = q.shape
    tk = k.shape[1]
    qb = min(SB_QBLOCK, tq)
    nb = -(-tq // qb)
    qp = jnp.pad(q, ((0, 0), (0, nb * qb - tq), (0, 0), (0, 0)))
    q_blocks = qp.reshape(b, nb, qb, h, d).transpose(1, 0, 2, 3, 4)
    pos_blocks = ((tk - tq) + jnp.arange(nb * qb)).reshape(nb, qb)
    out = lax.map(lambda blk: _sb_block(blk[0], k, v, blk[1], bias), (q_blocks, pos_blocks))
    return out.transpose(1, 0, 2, 3, 4).reshape(b, nb * qb, h, d)[:, :tq]


def _gla_chunk(state, inp):
    q, k, v, g = inp
    c = q.shape[2]
    b = jnp.cumsum(g, axis=2)
    causal = jnp.tril(jnp.ones((c, c), dtype=bool))[None, None, :, :, None]
    diff = b[:, :, :, None, :] - b[:, :, None, :, :]
    decay = jnp.where(causal, jnp.exp(jnp.where(causal, diff, 0.0)), 0.0)
    scores = jnp.einsum('bhtk,bhsk,bhtsk->bhts', q, k, decay)
    out = jnp.einsum('bhts,bhsv->bhtv', scores, v) + jnp.einsum('bhtk,bhkv->bhtv', q * jnp.exp(b), state)
    b_last = b[:, :, -1:, :]
    new_state = (jnp.exp(b_last[:, :, 0, :])[..., None] * state
                 + jnp.einsum('bhsk,bhsv->bhkv', k * jnp.exp(b_last - b), v))
    return new_state, out


def _gla(q, k, v, g, s0):
    bsz, L = q.shape[:2]
    c = min(GLA_CHUNK, L)
    n = -(-L // c)

    def to_chunks(a):
        a = jnp.pad(a.astype(jnp.float32), ((0, 0), (0, n * c - L), (0, 0), (0, 0)))
        return a.reshape(bsz, n, c, a.shape[2], a.shape[3]).transpose(1, 0, 3, 2, 4)

    s, o = lax.scan(_gla_chunk, s0.astype(jnp.float32), (to_chunks(q), to_chunks(k), to_chunks(v), to_chunks(g)))
    o = o.transpose(1, 0, 3, 2, 4).reshape(bsz, n * c, GLA_HEADS, GLA_DV)[:, :L]
    return o, s


def _layer(x, c, past_k, past_v, gla_s0, conv_buf0,
           norm1_g, norm2_g, w_mod, b_mod, w_in, q_norm_g, k_norm_g, sb_bias, w_gk_up, b_gk, gla_onorm_g,
           w_br_sb, w_br_gla, w_out, w_up, conv_w, conv_b, w_down):
    bsz, L, _ = x.shape
    mod = jax.nn.silu(c) @ w_mod + b_mod
    sh1, sc1, gt1, sh2, sc2, gt2 = jnp.split(mod[:, None, :], 6, axis=-1)
    h = _rmsnorm(x, norm1_g) * (1.0 + sc1) + sh1
    sq, sk, sv, gq, gk, gv, gr, gd, m_sb, m_gla = _split_last(h @ w_in, IN_SIZES)
    sq = _rmsnorm(sq.reshape(bsz, L, SB_HEADS, SB_HEAD_DIM), q_norm_g)
    sk = _rmsnorm(sk.reshape(bsz, L, SB_HEADS, SB_HEAD_DIM), k_norm_g)
    sv = sv.reshape(bsz, L, SB_HEADS, SB_HEAD_DIM)
    keys = jnp.concatenate([past_k.astype(sk.dtype), sk], axis=1)
    vals = jnp.concatenate([past_v.astype(sv.dtype), sv], axis=1)
    o_sb = _sb_attention(sq, keys, vals, sb_bias).reshape(bsz, L, SB_WIDTH).astype(x.dtype)
    log_a = jax.nn.log_sigmoid((gd @ w_gk_up + b_gk).astype(jnp.float32)) / GLA_TAU
    to_heads = lambda a: a.reshape(bsz, L, GLA_HEADS, GLA_DK)
    o_gla, s_new = _gla(to_heads(gq) * (GLA_DK ** -0.5), to_heads(gk),
                        gv.reshape(bsz, L, GLA_HEADS, GLA_DV), to_heads(log_a), gla_s0)
    o_gla = _rmsnorm(o_gla.astype(x.dtype), gla_onorm_g).reshape(bsz, L, GLA_V_WIDTH) * jax.nn.silu(gr)
    mixed = jax.nn.sigmoid(m_sb) * (o_sb @ w_br_sb) + jax.nn.sigmoid(m_gla) * (o_gla @ w_br_gla)
    x = x + gt1 * (mixed @ w_out)
    h2 = _rmsnorm(x, norm2_g) * (1.0 + sc2) + sh2
    u_gate, u_val = jnp.split(h2 @ w_up, 2, axis=-1)
    padded = jnp.concatenate([conv_buf0.astype(u_gate.dtype), u_gate], axis=1)
    conv = conv_b + sum(conv_w[i] * padded[:, i:i + L] for i in range(CONV_W))
    x = x + gt2 * ((jax.nn.silu(conv) * u_val) @ w_down)
    return x, sk, sv, s_new.astype(x.dtype), padded[:, L:]


def setup_inputs(seed: int = 0) -> dict:
    key = jax.random.key(seed)
    ks = jax.random.split(key, 32)
    n_pages = PAST_LEN // PAGE_SIZE
    n_used = DEC_BATCH * n_pages
    n_pool = n_used + max(1, n_used // 4)
    nrm = lambda k, shape, scale: scale * jax.random.normal(k, shape, jnp.float32)
    page_table = jax.random.permutation(ks[6], n_pool)[:n_used].reshape(DEC_BATCH, n_pages).astype(jnp.int32)
    return {
        'x_prompt': nrm(ks[0], (BATCH, SEQ, D_MODEL), 1.0),
        'x_sample': nrm(ks[1], (DEC_BATCH, DEC_SEQ, D_MODEL), 1.0),
        'c_prompt': nrm(ks[2], (BATCH, D_MODEL), 1.0),
        'c_sample': nrm(ks[3], (DEC_BATCH, D_MODEL), 1.0),
        'cache_k_sb': nrm(ks[4], (DEPTH, n_pool, PAGE_SIZE, SB_HEADS, SB_HEAD_DIM), 1.0),
        'cache_v_sb': nrm(ks[5], (DEPTH, n_pool, PAGE_SIZE, SB_HEADS, SB_HEAD_DIM), 1.0),
        'page_table': page_table,
        'state_gla': nrm(ks[7], (DEPTH, DEC_BATCH, GLA_HEADS, GLA_DK, GLA_DV), 1.0),
        'state_ffn_conv': nrm(ks[8], (DEPTH, DEC_BATCH, CONV_W - 1, D_FF), 1.0),
        'norm1_g': 1.0 + nrm(ks[9], (DEPTH, D_MODEL), 0.02),
        'norm2_g': 1.0 + nrm(ks[10], (DEPTH, D_MODEL), 0.02),
        'w_mod': nrm(ks[11], (DEPTH, D_MODEL, 6 * D_MODEL), 0.5 * D_MODEL ** -0.5),
        'b_mod': nrm(ks[12], (DEPTH, 6 * D_MODEL), 0.02),
        'w_in': nrm(ks[13], (DEPTH, D_MODEL, N_IN), D_MODEL ** -0.5),
        'q_norm_g': 1.0 + nrm(ks[14], (DEPTH, SB_HEAD_DIM), 0.02),
        'k_norm_g': 1.0 + nrm(ks[15], (DEPTH, SB_HEAD_DIM), 0.02),
        'sb_bias': SB_BIAS_INIT + nrm(ks[26], (DEPTH, SB_HEADS), 0.1),
        'w_gk_up': nrm(ks[16], (DEPTH, GLA_GATE_RANK, GLA_K_WIDTH), GLA_GATE_RANK ** -0.5),
        'b_gk': nrm(ks[17], (DEPTH, GLA_K_WIDTH), 0.02),
        'gla_onorm_g': 1.0 + nrm(ks[18], (DEPTH, GLA_DV), 0.02),
        'w_br_sb': nrm(ks[19], (DEPTH, SB_WIDTH, D_MODEL), SB_WIDTH ** -0.5),
        'w_br_gla': nrm(ks[20], (DEPTH, GLA_V_WIDTH, D_MODEL), GLA_V_WIDTH ** -0.5),
        'w_out': nrm(ks[21], (DEPTH, D_MODEL, D_MODEL), D_MODEL ** -0.5),
        'w_up': nrm(ks[22], (DEPTH, D_MODEL, 2 * D_FF), D_MODEL ** -0.5),
        'conv_w': nrm(ks[23], (DEPTH, CONV_W, D_FF), CONV_W ** -0.5),
        'conv_b': nrm(ks[24], (DEPTH, D_FF), 0.02),
        'w_down': nrm(ks[25], (DEPTH, D_FF, D_MODEL), D_FF ** -0.5),
    }


def reference(x_prompt, x_sample, c_prompt, c_sample, cache_k_sb, cache_v_sb, page_table, state_gla, state_ffn_conv,
              norm1_g, norm2_g, w_mod, b_mod, w_in, q_norm_g, k_norm_g, sb_bias, w_gk_up, b_gk, gla_onorm_g,
              w_br_sb, w_br_gla, w_out, w_up, conv_w, conv_b, w_down):
    xp, xs = x_prompt, x_sample
    bp = xp.shape[0]
    bs = xs.shape[0]
    kp_l, vp_l, sp_l, cp_l, ks_l, vs_l, ss_l, cs_l = [], [], [], [], [], [], [], []
    for l in range(DEPTH):
        weights = (norm1_g[l], norm2_g[l], w_mod[l], b_mod[l], w_in[l], q_norm_g[l], k_norm_g[l], sb_bias[l],
                   w_gk_up[l], b_gk[l], gla_onorm_g[l], w_br_sb[l], w_br_gla[l], w_out[l], w_up[l],
                   conv_w[l], conv_b[l], w_down[l])
        empty = jnp.zeros((bp, 0, SB_HEADS, SB_HEAD_DIM), xp.dtype)
        xp, kp, vp, sp, cp = _layer(xp, c_prompt, empty, empty,
                                    jnp.zeros((bp, GLA_HEADS, GLA_DK, GLA_DV), jnp.float32),
                                    jnp.zeros((bp, CONV_W - 1, D_FF), xp.dtype), *weights)
        past_k = cache_k_sb[l][page_table].reshape(bs, -1, SB_HEADS, SB_HEAD_DIM)
        past_v = cache_v_sb[l][page_table].reshape(bs, -1, SB_HEADS, SB_HEAD_DIM)
        xs, ks, vs, ss, cs = _layer(xs, c_sample, past_k, past_v, state_gla[l], state_ffn_conv[l], *weights)
        kp_l.append(kp); vp_l.append(vp); sp_l.append(sp); cp_l.append(cp)
        ks_l.append(ks); vs_l.append(vs); ss_l.append(ss); cs_l.append(cs)
    return (xp, xs, jnp.stack(kp_l), jnp.stack(vp_l), jnp.stack(sp_l), jnp.stack(cp_l),
            jnp.stack(ks_l), jnp.stack(vs_l), jnp.stack(ss_l), jnp.stack(cs_l))
```

```python
import numpy as np
import concourse.bass as bass
import concourse.mybir as mybir
from concourse.bass_utils import run_bass_kernel_spmd

F32 = mybir.dt.float32
BF16 = mybir.dt.bfloat16
I32 = mybir.dt.int32
AF = mybir.ActivationFunctionType
ALU = mybir.AluOpType
AX = mybir.AxisListType

ENGS = ("sp", "act", "dve", "pool", "pe")


class Op:
    __slots__ = ("eng", "fn", "deps", "marked", "sem", "val", "is_dma")

    def __init__(self, eng, fn, is_dma):
        self.eng = eng
        self.fn = fn
        self.deps = []
        self.marked = False
        self.sem = None
        self.val = 0
        self.is_dma = is_dma


class Region:
    __slots__ = ("name", "last_w", "readers", "dma_readers", "excl", "fence")

    def __init__(self, name, excl=False, fence=None):
        self.name = name
        self.fence = fence
        self.last_w = None
        self.readers = {}
        self.dma_readers = []
        self.excl = excl


class Prog:
    def __init__(self, nc, n_dma_sems=(24, 4, 24)):
        self.nc = nc
        self.ops = {e: [] for e in ENGS}
        self.eng_sem = {e: nc.alloc_semaphore("s_" + e) for e in ENGS}
        self.dma_sems = {
            "sp": [nc.alloc_semaphore(f"d_sp{i}") for i in range(n_dma_sems[0])],
            "act": [nc.alloc_semaphore(f"d_act{i}") for i in range(n_dma_sems[1])],
            "pool": [nc.alloc_semaphore(f"d_pool{i}") for i in range(n_dma_sems[2])],
        }
        self.dma_cnt = {q: [0] * len(v) for q, v in self.dma_sems.items()}
        self.dma_rr = {q: 0 for q in self.dma_sems}
        self.out_dmas = []
        self.n_ops = 0

    def region(self, name, excl=False, fence=None):
        return Region(name, excl, fence)

    def make_fence(self):
        f = []
        for e in ENGS:
            for op in reversed(self.ops[e]):
                if not op.is_dma:
                    f.append(op)
                    break
        latest = {}
        for q in self.dma_sems:
            for op in self.ops[q]:
                if op.is_dma:
                    latest[id(op.sem)] = op
        f.extend(latest.values())
        return f

    def _add(self, op, reads, writes):
        deps = []
        eng = op.eng
        wr = list(writes)
        for r in list(reads) + wr:
            if r.fence is not None:
                deps.extend(r.fence)
                r.fence = None
        for r in reads:
            if r.excl:
                wr.append(r)
                continue
            lw = r.last_w
            if lw is not None:
                if lw.is_dma or lw.eng != eng or eng != "pe":
                    deps.append(lw)
            if op.is_dma:
                r.dma_readers.append(op)
            else:
                r.readers[eng] = op
        for w in wr:
            lw = w.last_w
            if lw is not None:
                if lw.is_dma or op.is_dma or lw.eng != eng:
                    deps.append(lw)
                elif w.excl and eng != "pe":
                    deps.append(lw)
            for e2, ro in w.readers.items():
                if op.is_dma or e2 != eng:
                    deps.append(ro)
            deps.extend(w.dma_readers)
            w.last_w = op
            w.readers = {}
            w.dma_readers = []
        seen = set()
        for d in deps:
            if d is op or id(d) in seen:
                continue
            seen.add(id(d))
            op.deps.append(d)
            d.marked = True
        self.ops[eng].append(op)
        self.n_ops += 1
        return op

    def op(self, eng, fn, reads=(), writes=()):
        return self._add(Op(eng, fn, False), reads, writes)

    def dma(self, queue, fn, reads=(), writes=(), is_output=False):
        op = Op(queue, fn, True)
        k = self.dma_rr[queue]
        self.dma_rr[queue] = (k + 1) % len(self.dma_sems[queue])
        self.dma_cnt[queue][k] += 1
        op.sem = self.dma_sems[queue][k]
        op.val = 16 * self.dma_cnt[queue][k]
        op.marked = True
        self._add(op, reads, writes)
        if is_output:
            self.out_dmas.append(op)
        return op

    def emit(self):
        nc = self.nc
        fin = Op("sp", None, False)
        fin.deps = list(self.out_dmas)
        self.ops["sp"].append(fin)
        for e in ENGS:
            cnt = 0
            for op in self.ops[e]:
                if op.is_dma:
                    continue
                if op.marked:
                    cnt += 1
                    op.sem = self.eng_sem[e]
                    op.val = cnt

        def run(ename, eng):
            waited = {}
            for op in self.ops[ename]:
                for d in op.deps:
                    key = id(d.sem)
                    if waited.get(key, 0) >= d.val:
                        continue
                    waited[key] = d.val
                    eng.wait_ge(d.sem, d.val)
                if op.fn is None:
                    continue
                ins = op.fn(eng)
                if op.marked:
                    ins.then_inc(op.sem, 16 if op.is_dma else 1)

        with nc.Block() as block:
            @block.sync
            def _(e):
                run("sp", e)

            @block.scalar
            def _(e):
                run("act", e)

            @block.vector
            def _(e):
                run("dve", e)

            @block.gpsimd
            def _(e):
                run("pool", e)

            @block.tensor
            def _(e):
                run("pe", e)


class Tile:
    __slots__ = ("ap", "reg")

    def __init__(self, ap, reg):
        self.ap = ap
        self.reg = reg

    def __getitem__(self, key):
        return Tile(self.ap[key], self.reg)

    def re(self, pattern_, **kw):
        return Tile(self.ap.rearrange(pattern_, **kw), self.reg)

    def bc(self, shape):
        return Tile(self.ap.to_broadcast(list(shape)), self.reg)

    def un(self, axis):
        return Tile(self.ap.unsqueeze(axis), self.reg)

    def bitcast(self, dt):
        return Tile(self.ap.bitcast(dt), self.reg)


class Arena:
    def __init__(self, nc, prog, nbytes):
        self.prog = prog
        self.t = nc.alloc_sbuf_tensor("arena", [128, nbytes // 4], F32)
        self.ap = self.t.ap()
        self.cap = nbytes
        self.ranges = [(0, nbytes)]
        self.tops = [0]
        self.marks = []
        self.fence = None

    def set_ranges(self, ranges):
        self.ranges = [tuple(r) for r in ranges]
        self.tops = [r[0] for r in ranges]
        self.marks = []
        self.fence = self.prog.make_fence()

    def mark(self):
        self.marks.append(list(self.tops))

    def release(self):
        self.tops = self.marks.pop()
        self.fence = self.prog.make_fence()

    def _tile(self, name, off, nb, cols, dtype, parts):
        assert off % 4 == 0 and off + nb <= self.cap
        a = self.ap[:, off // 4:(off + nb) // 4]
        if dtype != F32:
            a = a.bitcast(dtype)
        return Tile(a[:parts, :cols], self.prog.region(name, fence=self.fence))

    def alloc(self, name, cols, dtype=F32, parts=128, rng=None):
        esz = 2 if dtype == BF16 else 4
        nb = (cols * esz + 63) // 64 * 64
        for i, (lo, hi) in enumerate(self.ranges):
            if rng is not None and i != rng:
                continue
            if self.tops[i] + nb <= hi:
                off = self.tops[i]
                self.tops[i] += nb
                return self._tile(name, off, nb, cols, dtype, parts)
        raise AssertionError(f"SBUF overflow at {name} ({nb} B): ranges {self.ranges} tops {self.tops}")

    def fixed(self, name, cols, dtype, off, parts=128):
        esz = 2 if dtype == BF16 else 4
        nb = (cols * esz + 63) // 64 * 64
        return self._tile(name, off, nb, cols, dtype, parts)


def _regs(ts):
    return [t.reg for t in ts if isinstance(t, Tile) and t.reg is not None]


class B:
    def __init__(self, P):
        self.P = P

    def dma(self, q, out, in_, is_output=False):
        o, i = out.ap, in_.ap
        if q == "pool_ind":
            raise ValueError
        return self.P.dma(q, lambda e: e.dma_start(out=o, in_=i), _regs([in_]), _regs([out]), is_output)

    def gather(self, out, table, idx):
        o, t, ix = out.ap, table.ap, idx.ap
        return self.P.dma(
            "pool",
            lambda e: e.indirect_dma_start(out=o, out_offset=None, in_=t,
                                           in_offset=bass.IndirectOffsetOnAxis(ap=ix, axis=0)),
            _regs([idx, table]), _regs([out]))

    def act(self, out, in_, func, bias=None, scale=None, accum=None):
        kw = {}
        rd = [in_]
        wr = [out]
        if bias is not None:
            kw["bias"] = bias.ap if isinstance(bias, Tile) else bias
            rd.append(bias)
        if scale is not None:
            kw["scale"] = scale.ap if isinstance(scale, Tile) else scale
            rd.append(scale)
        if accum is not None:
            kw["accum_out"] = accum.ap
            wr.append(accum)
            rd.append(accum)
        o, i = out.ap, in_.ap
        return self.P.op("act", lambda e: e.activation(out=o, in_=i, func=func, **kw), _regs(rd), _regs(wr))

    def ts(self, out, in0, s1, s2=None, op0=ALU.mult, op1=None, eng="dve"):
        a1 = s1.ap if isinstance(s1, Tile) else s1
        a2 = s2.ap if isinstance(s2, Tile) else s2
        o, i = out.ap, in0.ap
        if op1 is None:
            fn = lambda e: e.tensor_scalar(out=o, in0=i, scalar1=a1, scalar2=None, op0=op0)
        else:
            fn = lambda e: e.tensor_scalar(out=o, in0=i, scalar1=a1, scalar2=a2, op0=op0, op1=op1)
        return self.P.op(eng, fn, _regs([in0, s1, s2]), _regs([out]))

    def tt(self, out, in0, in1, op, eng="dve"):
        o, a, b = out.ap, in0.ap, in1.ap
        return self.P.op(eng, lambda e: e.tensor_tensor(out=o, in0=a, in1=b, op=op), _regs([in0, in1]), _regs([out]))

    def stt(self, out, in0, scalar, in1, op0, op1):
        o, a, b = out.ap, in0.ap, in1.ap
        s = scalar.ap if isinstance(scalar, Tile) else scalar
        return self.P.op("dve", lambda e: e.scalar_tensor_tensor(out=o, in0=a, scalar=s, in1=b, op0=op0, op1=op1),
                         _regs([in0, in1, scalar]), _regs([out]))

    def copy(self, out, in_, eng="dve"):
        o, i = out.ap, in_.ap
        if eng == "act":
            return self.P.op("act", lambda e: e.activation(out=o, in_=i, func=AF.Copy), _regs([in_]), _regs([out]))
        return self.P.op(eng, lambda e: e.tensor_copy(out=o, in_=i), _regs([in_]), _regs([out]))

    def memset(self, t, val, eng="pool"):
        o = t.ap
        return self.P.op(eng, lambda e: e.memset(o, val), [], _regs([t]))

    def reduce(self, out, in_, op=ALU.add, eng="dve"):
        o, i = out.ap, in_.ap
        return self.P.op(eng, lambda e: e.tensor_reduce(out=o, in_=i, axis=AX.X, op=op), _regs([in_]), _regs([out]))

    def mm(self, out, lhsT, rhs, start=True, stop=True, skip=False):
        o, l, r = out.ap, lhsT.ap, rhs.ap
        return self.P.op("pe", lambda e: e.matmul(o, lhsT=l, rhs=r, start=start, stop=stop, skip_group_check=skip),
                         _regs([lhsT, rhs]), _regs([out]))

    def tr(self, out, in_, ident):
        o, i, d = out.ap, in_.ap, ident.ap
        return self.P.op("pe", lambda e: e.transpose(out=o, in_=i, identity=d), _regs([in_, ident]), _regs([out]))


D = 2048
KC = 16
NT = 9
NP_ = 7
T = NT * 128
TPRE = NP_ * 128
TS = 16
TT = T + TS
NH = 16
HD = 64
DFF = 5632
NFF = DFF // 128
N_IN = 10256
OFF_Q, OFF_K, OFF_V = 0, 1024, 2048
OFF_GQ, OFF_GK, OFF_GV, OFF_GR, OFF_GD = 3072, 3584, 4096, 5120, 6144
OFF_MSB, OFF_MGLA = 6160, 8208
NPAGES = 64
NPOOL_ROWS = 2560 * 128
EPS = 1e-6
PENALTY = -30000.0

C_ID, C_TRIS, C_ONES, C_MDIAG, C_MINC, C_MSTL, C_MC = 0, 128, 256, 384, 512, 640, 768
C_IOTA = 896
C_MINC16, C_MSTL16, C_MC16 = 897, 913, 929
C_SEGC = 945
C_SEGR = 1009
C_M65 = 1013
C_TRI8 = 1077
C_ONE8 = 1205
C_TOT = 1333


def make_consts():
    c = np.zeros((128, C_TOT), np.float32)
    s = np.arange(128)[:, None]
    t = np.arange(128)[None, :]
    c[:, C_ID:C_ID + 128] = (s == t)
    c[:, C_TRIS:C_TRIS + 128] = (s > t)
    c[:, C_ONES:C_ONES + 128] = 1.0
    c[:, C_MDIAG:C_MDIAG + 128] = (s < t)
    c[:, C_MINC:C_MINC + 128] = (s <= t) * (-1.0 / 16.0)
    c[:, C_MSTL:C_MSTL + 128] = (s > t) * (-1.0 / 16.0)
    c[:, C_MC:C_MC + 128] = (s <= t)
    c[:, C_IOTA] = np.arange(128)
    s16 = np.arange(16)[:, None]
    t16 = np.arange(16)[None, :]
    same = (s16 // 4) == (t16 // 4)
    c[:16, C_MINC16:C_MINC16 + 16] = (same & (s16 <= t16)) * (-1.0 / 16.0)
    c[:16, C_MSTL16:C_MSTL16 + 16] = (same & (s16 > t16)) * (-1.0 / 16.0)
    c[:16, C_MC16:C_MC16 + 16] = (same & (s16 <= t16))
    for i in range(4):
        c[:, C_SEGC + i * 16 + 4 * i:C_SEGC + i * 16 + 4 * i + 4] = 1.0
        c[4 * i:4 * i + 4, C_SEGR + i] = 1.0
    for col in range(64):
        tq = col % 4
        c[:tq, C_M65 + col] = 1.0
    c[:, C_TRI8:C_TRI8 + 128] = (s >= t) * (-8.0)
    c[:, C_ONE8:C_ONE8 + 128] = -8.0
    return c


def build_program(stop_after=None):
    nc = bass.Bass("TRN2", target_bir_lowering=False)
    P = Prog(nc)
    b = B(P)
    A = Arena(nc, P, 206 * 1024)

    def din(name, shape, dt=F32):
        return Tile(nc.dram_tensor(name, list(shape), dt, kind="ExternalInput").ap(), None)

    def dout(name, shape, dt=F32):
        return Tile(nc.dram_tensor(name, list(shape), dt, kind="ExternalOutput").ap(), P.region(name))

    def dscr(name, shape, dt=F32):
        return Tile(nc.dram_tensor(name, list(shape), dt, kind="Internal").ap(), P.region(name))

    xm_d = din("xm", [T, D])
    xp_d = din("xp", [TPRE, D])
    xs_d = din("xs", [TS, D])
    flg_d = din("flg", [128, 2])
    ccT_d = din("ccT", [D, 5])
    pt_d = din("pt", [1, 4 * NPAGES], I32)
    ck_d = din("ck", [NPOOL_ROWS, 1024])
    cv_d = din("cv", [NPOOL_ROWS, 1024])
    sg_d = din("sg", [4, 4, 128, 256])
    sc_d = din("sc", [8, DFF])
    cst_d = din("cst", [128, C_TOT])
    n1g_d = din("norm1_g", [16, 128])
    n2g_d = din("norm2_g", [16, 128])
    wmod_d = din("w_mod", [D, 6 * D])
    bmod_d = din("b_mod", [96, 128])
    bmodr_d = din("b_mod_row", [1, 6 * D])
    win_d = din("w_in", [D, N_IN])
    qng_d = din("q_norm_g", [1, HD])
    kng_d = din("k_norm_g", [1, HD])
    sbb_d = din("sb_bias", [1, NH])
    wgk_d = din("w_gk_up", [16, 512])
    bgk_d = din("b_gk", [1, 512])
    gon_d = din("gla_onorm_g", [1, 256])
    wbs_d = din("w_br_sb", [1024, D])
    wbg_d = din("w_br_gla", [1024, D])
    wout_d = din("w_out", [D, D])
    wup_d = din("w_up", [D, 2 * DFF])
    cw_d = din("conv_w", [132, 128])
    cb_d = din("conv_b", [44, 128])
    wdn_d = din("w_down", [DFF, D])

    y_d = dout("y", [T, D])
    ys_d = dout("ys", [TS, D])
    kp_d = dout("kp", [T, 1024])
    vp_d = dout("vp", [T, 1024])
    sgp_d = dout("sgp", [4, 128, 256])
    cvp_d = dout("cvp", [2, DFF])
    ks_d = dout("ks", [TS, 1024])
    vs_d = dout("vs", [TS, 1024])
    sgs_d = dout("sgs", [4, 4, 128, 256])
    cvs_d = dout("cvs", [8, DFF])
    gate_d = dscr("gate_scr", [2, 5, D])

    banks = []
    for i in range(8):
        t = nc.alloc_psum_tensor(f"ps{i}", [128, 512], F32)
        banks.append(Tile(t.ap(), P.region(f"ps{i}", excl=True)))
    bank_rr = [0]

    def next_bank():
        k = bank_rr[0]
        bank_rr[0] = (k + 1) % 8
        return banks[k]

    cst = A.alloc("cst", C_TOT)
    cbf = A.alloc("cbf", 512 + 64 + 256, BF16)
    flg = A.alloc("flg", 2)
    modv = A.alloc("modv", 4 * KC * 5)
    scT = A.alloc("scT", KC * 5, BF16)
    b1p = A.alloc("b1p", 32)
    colT = A.alloc("colT", 128)
    cwT = A.alloc("cwT", 176)
    bias_all = A.alloc("bias_all", NH)
    bias_pre = A.alloc("bias_pre", NH)
    bias_ht = A.alloc("bias_ht", 64)
    qg_b = A.alloc("qg_b", HD)
    kg_b = A.alloc("kg_b", HD)
    gon_b = A.alloc("gon_b", 256)
    wgk = A.alloc("wgk", 512, BF16, parts=32)

    ident_f = cst[:, C_ID:C_ID + 128]
    ident_b = cbf[:, 0:128]
    triS_b = cbf[:, 128:256]
    ones_b = cbf[:, 256:384]
    mdiag_b = cbf[:, 384:512]
    m65_b = cbf[:, 512:576]
    tri8_b = cbf[:, 576:704]
    one8_b = cbf[:, 704:832]
    mdiag_f = cst[:, C_MDIAG:C_MDIAG + 128]
    m65_f = cst[:, C_M65:C_M65 + 64]
    modv4 = modv.re("p (v k b) -> p v k b", v=4, k=KC)

    b.dma("sp", cst, cst_d)
    b.dma("sp", flg, flg_d)
    b.copy(cbf[:, 0:512], cst[:, 0:512], "dve")
    b.copy(m65_b, m65_f, "dve")
    b.copy(cbf[:, 576:832], cst[:, C_TRI8:C_TRI8 + 256], "dve")
    b.dma("sp", bias_all, Tile(sbb_d.ap[0, :].partition_broadcast(128), None))
    b.dma("sp", qg_b, Tile(qng_d.ap[0, :].partition_broadcast(128), None))
    b.dma("sp", kg_b, Tile(kng_d.ap[0, :].partition_broadcast(128), None))
    b.dma("sp", gon_b, Tile(gon_d.ap[0, :].partition_broadcast(128), None))
    b.ts(bias_pre, bias_all, flg[:, 0:1], op0=ALU.add)
    b.copy(bias_ht.re("p (h t) -> p h t", t=4), bias_all.un(2).bc([128, NH, 4]), "dve")

    A.mark()
    rows = A.alloc("rows", 128)
    b.dma("sp", rows[0:96, :], bmod_d)
    b.dma("sp", rows[96:112, :], n1g_d)
    b.dma("sp", rows[112:128, :], n2g_d)
    bk = next_bank()
    b.tr(bk[:, 0:128], rows, ident_f)
    b.copy(colT, bk[:, 0:128], "dve")
    rowsA = A.alloc("rowsA", 128)
    rowsB = A.alloc("rowsB", 128)
    b.dma("sp", rowsA, Tile(cw_d.ap[0:128, :], None))
    b.dma("sp", rowsB[0:4, :], Tile(cw_d.ap[128:132, :], None))
    b.dma("sp", rowsB[4:48, :], cb_d)
    bk = next_bank()
    b.tr(bk[:, 0:128], rowsA, ident_f)
    b.tr(bk[:, 128:176], rowsB[0:48, :], ident_f[0:48, 0:48])
    b.copy(cwT, bk[:, 0:176], "dve")
    wgk_f = A.alloc("wgk_f", 512, F32, parts=32)
    b.dma("sp", wgk_f[0:16, :], wgk_d)
    b.dma("sp", wgk_f[16:17, :], bgk_d)
    b.copy(wgk[0:17, :], wgk_f[0:17, :], "dve")
    c_sb = A.alloc("c_sb", KC * 5)
    b.dma("sp", c_sb.re("p (k b) -> p k b", k=KC), Tile(ccT_d.ap.rearrange("(k p) b -> p k b", p=128), None))
    b.act(scT, c_sb, AF.Silu)
    scT3 = scT.re("p (k b) -> p k b", k=KC)
    b.ts(b1p[:, 0:16], colT[:, 16:32], 1.0, op0=ALU.add)
    b.ts(b1p[:, 16:32], colT[:, 64:80], 1.0, op0=ALU.add)
    def mod_load(j, c0, ncols, W, brow_t):
        W3 = W[:, 0:KC * ncols].re("p (k n) -> p k n", k=KC)
        b.dma("pool", W3, Tile(wmod_d.ap[:, j * D + c0:j * D + c0 + ncols].rearrange("(k p) n -> p k n", p=128), None))
        if j in (2, 5):
            b.dma("sp", brow_t[:, 0:ncols], Tile(bmodr_d.ap[0, j * D + c0:j * D + c0 + ncols].partition_broadcast(5), None))

    def mod_block(j, c0, ncols, W, brow_t, gst_t, bk, load=True):
        W3 = W[:, 0:KC * ncols].re("p (k n) -> p k n", k=KC)
        if load:
            mod_load(j, c0, ncols, W, brow_t)
        if j in (0, 1, 3, 4):
            vi = {1: 0, 0: 1, 4: 2, 3: 3}[j]
            for fc in range(ncols // 128):
                kcf = c0 // 128 + fc
                col = bk[:, fc * 8:fc * 8 + 5]
                for kc in range(KC):
                    b.mm(col, W3[:, kc, fc * 128:(fc + 1) * 128], scT3[:, kc, :],
                         start=(kc == 0), stop=(kc == KC - 1))
                if j == 1:
                    b.ts(modv4[:, vi, kcf, :], col, b1p[:, kcf:kcf + 1], colT[:, 96 + kcf:97 + kcf],
                         op0=ALU.add, op1=ALU.mult)
                elif j == 4:
                    b.ts(modv4[:, vi, kcf, :], col, b1p[:, 16 + kcf:17 + kcf],
                         colT[:, 112 + kcf:113 + kcf], op0=ALU.add, op1=ALU.mult)
                else:
                    b.ts(modv4[:, vi, kcf, :], col, colT[:, j * 16 + kcf:j * 16 + kcf + 1], op0=ALU.add)
        else:
            gi = 0 if j == 2 else 1
            for kc in range(KC):
                b.mm(bk[0:5, 0:ncols], scT3[:, kc, :], W3[:, kc, :], start=(kc == 0), stop=(kc == KC - 1))
            b.tt(gst_t[:, 0:ncols], bk[0:5, 0:ncols], brow_t[:, 0:ncols], ALU.add)
            b.dma("sp", Tile(gate_d.ap[gi, :, c0:c0 + ncols], gate_d.reg), gst_t[:, 0:ncols])

    wmb = [A.alloc(f"wmb{i}", KC * 512, BF16) for i in range(2)]
    blk_i = 0
    for j in (1, 0):
        for q in range(4):
            mod_block(j, q * 512, 512, wmb[blk_i % 2], None, None, next_bank())
            blk_i += 1
    deferred_mod = [(j, q * 256) for j in (4, 3, 2, 5) for q in range(8)]
    A.release()
    S1, SH1, S2, SH2 = 0, 1, 2, 3

    wk = {}
    SGROUPS = [(0, 4, 1), (4, 8, 2), (8, 12, 3), (12, 16, 4)]
    PGROUP = [(0, 128, 0)]

    def norm_stats(xt, n, pfx, slot):
        junk = wk[pfx + "junk"]
        st = wk[pfx + "st"][slot % 2]
        xn = wk[pfx + "xn"][slot % 2]
        b.memset(st[:n, 0:1], 0.0, "dve")
        b.act(junk[:n, :], xt[:n, :], AF.Square, accum=st[:n, 0:1])
        b.act(st[:n, 1:2], st[:n, 0:1], AF.Ln, scale=1.0 / D, bias=EPS)
        b.act(st[:n, 2:3], st[:n, 1:2], AF.Exp, scale=-0.5)
        b.ts(xn[:n, :], xt[:n, :], st[:n, 2:3], op0=ALU.mult)

    def norm_transpose(n, si, shi, outT, col0, groups, pfx, slot, bankpair=None):
        xn = wk[pfx + "xn"][slot % 2]
        bkA, bkB = bankpair if bankpair is not None else (next_bank(), next_bank())
        for kc in range(KC):
            bk = bkA if kc < 8 else bkB
            pb = bk.bitcast(BF16)
            k8 = kc % 8
            b.tr(pb[:, k8 * 128:k8 * 128 + n], xn[:n, kc * 128:(kc + 1) * 128], ident_b[:n, :n])
        ei = 0
        for kc in range(KC):
            bk = bkA if kc < 8 else bkB
            pb = bk.bitcast(BF16)
            k8 = kc % 8
            for (lo, hi, bi) in groups:
                src = pb[:, k8 * 128 + lo:k8 * 128 + hi]
                dst = outT[:, kc, col0 + lo:col0 + hi]
                sc_ = modv4[:, si, kc, bi:bi + 1]
                sh_ = modv4[:, shi, kc, bi:bi + 1]
                if ei % 2 == 0:
                    b.act(dst, src, AF.Identity, bias=sh_, scale=sc_)
                else:
                    b.ts(dst, src, sc_, sh_, op0=ALU.mult, op1=ALU.add)
                ei += 1

    def norm_pipeline(tiles, load_fn, si, shi, outT, pfx, xbuf):
        def s1(t):
            n, col0, groups = tiles[t]
            xt = xbuf[t % len(xbuf)]
            load_fn(t, xt)
            norm_stats(xt, n, pfx, t)
        s1(0)
        for t in range(len(tiles)):
            if t + 1 < len(tiles):
                s1(t + 1)
            n, col0, groups = tiles[t]
            norm_transpose(n, si, shi, outT, col0, groups, pfx, t)

    def load_w_block(W, src_d, c0, ncols, nk=KC, r0=0):
        W3 = W[:, 0:nk * ncols].re("p (k n) -> p k n", k=nk)
        b.dma("pool", W3, Tile(src_d.ap[r0:r0 + nk * 128, c0:c0 + ncols].rearrange("(k p) n -> p k n", p=128), None))
        return W3

    def alloc_norm_work(pfx, rngs=(None, None, None)):
        wk[pfx + "junk"] = A.alloc(pfx + "junk", D, BF16, rng=rngs[0])
        wk[pfx + "st"] = [A.alloc(pfx + f"st{i}", 4) for i in range(2)]
        wk[pfx + "xn"] = [A.alloc(pfx + f"xn{i}", D, BF16, rng=rngs[1 + i]) for i in range(2)]

    def alloc_qk_work(pfx):
        wk[pfx + "sq"] = A.alloc(pfx + "sq", 512)
        wk[pfx + "st8"] = A.alloc(pfx + "st8", 24)
        wk[pfx + "tmp"] = A.alloc(pfx + "tmp", 512)

    def qk_post(ps, n, gtile, out_f32, out_bf, pfx):
        sq = wk[pfx + "sq"]
        st8 = wk[pfx + "st8"]
        tmp = wk[pfx + "tmp"]
        b.act(sq[:n, :], ps[:n, :], AF.Square)
        b.reduce(st8[:n, 0:8], sq[:n, :].re("p (h d) -> p h d", h=8))
        b.act(st8[:n, 8:16], st8[:n, 0:8], AF.Ln, scale=1.0 / HD, bias=EPS)
        b.act(st8[:n, 16:24], st8[:n, 8:16], AF.Exp, scale=-0.5)
        b.tt(tmp[:n, :].re("p (h d) -> p h d", h=8), ps[:n, :].re("p (h d) -> p h d", h=8),
             st8[:n, 16:24].un(2).bc([n, 8, HD]), ALU.mult)
        gb = gtile[:n, :].un(1).bc([n, 8, HD])
        if out_f32 is not None:
            b.tt(out_f32.re("p (h d) -> p h d", h=8), tmp[:n, :].re("p (h d) -> p h d", h=8), gb, ALU.mult)
            b.copy(out_bf, out_f32, "pool")
        else:
            b.tt(out_bf.re("p (h d) -> p h d", h=8), tmp[:n, :].re("p (h d) -> p h d", h=8), gb, ALU.mult)

    def transpose_to(dst3, col0, src_bf, n, nchunks, c_off, eng, bank=None):
        bk = bank if bank is not None else next_bank()
        pb = bk.bitcast(BF16)
        for c in range(nchunks):
            b.tr(pb[:, c * 128:c * 128 + n], src_bf[:n, c * 128:(c + 1) * 128], ident_b[:n, :n])
        src = pb[:, 0:nchunks * 128].re("p (c n) -> p c n", c=nchunks)[:, :, 0:n]
        b.copy(dst3[:, c_off:c_off + nchunks, col0:col0 + n], src, eng)

    def proj_tm(hT3, tiles, src_d, c0, ncols, consume, wbufs, blk):
        nb = (ncols + blk - 1) // blk
        pend = []
        Wn = load_w_block(wbufs[0], src_d, c0, min(blk, ncols))
        for bi in range(nb):
            cw = min(blk, ncols - bi * blk)
            W3 = Wn
            if bi + 1 < nb:
                Wn = load_w_block(wbufs[(bi + 1) % len(wbufs)], src_d, c0 + (bi + 1) * blk,
                                  min(blk, ncols - (bi + 1) * blk))
            for (col0, n, tag) in tiles:
                bk = next_bank()
                for kc in range(KC):
                    b.mm(bk[:n, 0:cw], hT3[:, kc, col0:col0 + n], W3[:, kc, :], start=(kc == 0), stop=(kc == KC - 1))
                pend.append((bi * blk, cw, tag, n, bk))
                if len(pend) > 2:
                    consume(*pend.pop(0))
        while pend:
            consume(*pend.pop(0))

    def proj_fm(hT3, ntok, src_d, c0, ncols, consume, wbufs, blk):
        nb = (ncols + blk - 1) // blk
        Wn = load_w_block(wbufs[0], src_d, c0, min(blk, ncols))
        for bi in range(nb):
            cwb = min(blk, ncols - bi * blk)
            W3 = Wn
            if bi + 1 < nb:
                Wn = load_w_block(wbufs[(bi + 1) % len(wbufs)], src_d, c0 + (bi + 1) * blk,
                                  min(blk, ncols - (bi + 1) * blk))
            for ci in range((cwb + 127) // 128):
                cw = min(128, cwb - ci * 128)
                for t0 in range(0, ntok, 512):
                    tw = min(512, ntok - t0)
                    bk = next_bank()
                    for kc in range(KC):
                        b.mm(bk[:cw, 0:tw], W3[:, kc, ci * 128:ci * 128 + cw], hT3[:, kc, t0:t0 + tw],
                             start=(kc == 0), stop=(kc == KC - 1))
                    consume((bi * blk) // 128 + ci, t0, tw, cw, bk)

    def gla_pass(pfx, hT3, ntok, tiles, want_out, states, wbufs, blk, oT3=None):
        gdT = A.alloc(pfx + "gdT", ntok, BF16, parts=32)
        b.memset(gdT, 1.0, "pool")
        ntl = len(tiles)
        sp_tok = A.alloc(pfx + "sp", ntl * 512)
        gk_tok = A.alloc(pfx + "gk", ntl * 512, BF16)
        gv_tok = A.alloc(pfx + "gv", ntl * 1024, BF16)
        sp3 = sp_tok.re("p (t n) -> p t n", t=ntl)
        gk3 = gk_tok.re("p (t n) -> p t n", t=ntl)
        gv3 = gv_tok.re("p (t n) -> p t n", t=ntl)
        if want_out:
            gqT = A.alloc(pfx + "gqT", 4 * ntok, BF16)
            gkT = A.alloc(pfx + "gkT", 4 * ntok, BF16)
            gqT3 = gqT.re("p (h n) -> p h n", h=4)
            gkT3 = gkT.re("p (h n) -> p h n", h=4)
            grs = A.alloc(pfx + "grs", ntl * 1024, BF16)
            grs3 = grs.re("p (t n) -> p t n", t=ntl)
        wsm = A.alloc(pfx + "wsm", KC * 16, BF16)
        ebuf = A.alloc(pfx + "ebuf", 512)
        W3 = load_w_block(wsm, win_d, OFF_GD, 16)
        for t0 in range(0, ntok, 512):
            tw = min(512, ntok - t0)
            bk = next_bank()
            for kc in range(KC):
                b.mm(bk[:16, 0:tw], W3[:, kc, :], hT3[:, kc, t0:t0 + tw], start=(kc == 0), stop=(kc == KC - 1))
            b.copy(gdT[0:16, t0:t0 + tw], bk[:16, 0:tw], "act")
        for li, tl in enumerate(tiles):
            n, c0 = tl["n"], tl["col0"]
            bk = next_bank()
            b.mm(bk[:n, :], gdT[0:17, c0:c0 + n], wgk[0:17, :])
            b.act(ebuf[:n, :], bk[:n, :], AF.Exp, scale=-1.0)
            b.act(sp3[:n, li, :], ebuf[:n, :], AF.Ln, scale=1.0, bias=1.0)
        tm_tiles = [(tl["col0"], tl["n"], li) for li, tl in enumerate(tiles)]

        def cons_gk(cofs, cw, li, n, bk):
            b.copy(gk3[:n, li, cofs:cofs + cw], bk[:n, 0:cw], "act")

        def cons_gv(cofs, cw, li, n, bk):
            b.copy(gv3[:n, li, cofs:cofs + cw], bk[:n, 0:cw], "act" if (cofs // blk) % 2 else "dve")

        proj_tm(hT3, tm_tiles, win_d, OFF_GK, 512, cons_gk, wbufs, blk)
        proj_tm(hT3, tm_tiles, win_d, OFF_GV, 1024, cons_gv, wbufs, blk)
        if want_out:
            def cons_gq(ci, t0, tw, cw, bk):
                b.act(gqT3[:, ci, t0:t0 + tw], bk[:, 0:tw], AF.Copy, scale=128.0 ** -0.5)

            def cons_gkT(ci, t0, tw, cw, bk):
                b.copy(gkT3[:, ci, t0:t0 + tw], bk[:, 0:tw], "dve")

            def cons_gr(cofs, cw, li, n, bk):
                b.act(grs3[:n, li, cofs:cofs + cw], bk[:n, 0:cw], AF.Silu)

            proj_fm(hT3, ntok, win_d, OFF_GQ, 512, cons_gq, wbufs, blk)
            proj_fm(hT3, ntok, win_d, OFF_GK, 512, cons_gkT, wbufs, blk)
            proj_tm(hT3, tm_tiles, win_d, OFF_GR, 1024, cons_gr, wbufs, blk)
        eb = A.alloc(pfx + "EbT", 512)
        el = A.alloc(pfx + "EL", 512)
        kp_ = A.alloc(pfx + "kp", 512, BF16)
        if want_out:
            en = A.alloc(pfx + "EnT", 512)
            qt = A.alloc(pfx + "qtT", 512, BF16)
            kt = A.alloc(pfx + "ktT", 512, BF16)
            pm = A.alloc(pfx + "Pm", 512, BF16)
            qseg = A.alloc(pfx + "qseg", 4 * 64, BF16)
            on = A.alloc(pfx + "on", 1024)
            onb = A.alloc(pfx + "onb", 1024, BF16)
            st4 = A.alloc(pfx + "st4", 12)
        kseg = A.alloc(pfx + "kseg", 512, BF16, parts=16)
        bkT, bkL, bkS = banks[0], banks[1], banks[2]
        bo = [banks[3], banks[4]]
        bs = [banks[5], banks[6]]
        for li, tl in enumerate(tiles):
            n, c0 = tl["n"], tl["col0"]
            minc, mstl, mc = tl["masks"]
            segs = tl["segs"]
            for hd in range(4):
                b.mm(bkT[:, hd * n:(hd + 1) * n], sp3[:n, li, hd * 128:(hd + 1) * 128], minc)
            b.mm(bkL[:n, :], mstl, sp3[:n, li, :])
            b.act(eb[:, 0:4 * n], bkT[:, 0:4 * n], AF.Exp)
            b.act(el[:n, :], bkL[:n, :], AF.Exp)
            b.tt(kp_[:n, :], gk3[:n, li, :], el[:n, :], ALU.mult)
            if want_out:
                b.act(en[:, 0:4 * n], bkT[:, 0:4 * n], AF.Exp, scale=-1.0)
                qt3 = qt[:, 0:4 * n].re("p (h n) -> p h n", h=4)
                kt3 = kt[:, 0:4 * n].re("p (h n) -> p h n", h=4)
                b.tt(qt3, gqT3[:, :, c0:c0 + n], eb[:, 0:4 * n].re("p (h n) -> p h n", h=4), ALU.mult)
                b.tt(kt3, gkT3[:, :, c0:c0 + n], en[:, 0:4 * n].re("p (h n) -> p h n", h=4), ALU.mult)
                for hd in range(4):
                    b.mm(bkS[:n, hd * n:(hd + 1) * n], kt3[:, hd, :], qt3[:, hd, :])
                pm3 = pm[:n, 0:4 * n].re("p (h n) -> p h n", h=4)
                b.tt(pm3, bkS[:n, 0:4 * n].re("p (h n) -> p h n", h=4), mc.un(1).bc([n, 4, n]), ALU.mult)
                if len(segs) > 1:
                    for si in range(len(segs)):
                        for hd in range(4):
                            b.tt(qseg[:, si * 64 + hd * 16:si * 64 + hd * 16 + 16], qt3[:, hd, :],
                                 cst[:, C_SEGC + si * 16:C_SEGC + si * 16 + 16], ALU.mult)
                for hd in range(4):
                    ob = bo[hd // 2][:n, (hd % 2) * 256:(hd % 2 + 1) * 256]
                    b.mm(ob, pm3[:, hd, :], gv3[:n, li, hd * 256:(hd + 1) * 256], start=True, stop=False)
                    for si, (lo, hi, sidx) in enumerate(segs):
                        if len(segs) == 1:
                            lq = qt3[:, hd, :]
                        else:
                            lq = qseg[:, si * 64 + hd * 16:si * 64 + hd * 16 + 16]
                        b.mm(ob, lq, states[sidx][1][:, hd * 256:(hd + 1) * 256], start=False,
                             stop=(si == len(segs) - 1))
                for hf in range(2):
                    b.act(on[:n, hf * 512:(hf + 1) * 512], bo[hf][:n, :], AF.Square)
                b.reduce(st4[:n, 0:4], on[:n, :].re("p (h d) -> p h d", h=4))
                b.act(st4[:n, 4:8], st4[:n, 0:4], AF.Ln, scale=1.0 / 256.0, bias=EPS)
                b.act(st4[:n, 8:12], st4[:n, 4:8], AF.Exp, scale=-0.5)
                for hf in range(2):
                    b.tt(on[:n, hf * 512:(hf + 1) * 512].re("p (h d) -> p h d", h=2),
                         bo[hf][:n, :].re("p (h d) -> p h d", h=2),
                         st4[:n, 8 + 2 * hf:10 + 2 * hf].un(2).bc([n, 2, 256]), ALU.mult)
                b.tt(on[:n, :].re("p (h d) -> p h d", h=4), on[:n, :].re("p (h d) -> p h d", h=4),
                     gon_b[:n, :].un(1).bc([n, 4, 256]), ALU.mult)
                b.tt(onb[:n, :], on[:n, :], grs3[:n, li, :], ALU.mult)
                transpose_to(oT3, c0, onb, n, 8, 0, "act", bank=banks[7])
            for si, (lo, hi, sidx) in enumerate(segs):
                S, Sb = states[sidx]
                if len(segs) == 1:
                    kps = kp_
                else:
                    kps = kseg
                    b.ts(kseg[:n, :], kp_[:n, :], cst[:n, C_SEGR + si:C_SEGR + si + 1], op0=ALU.mult)
                for hd in range(4):
                    sb_ = bs[hd // 2][:, (hd % 2) * 256:(hd % 2 + 1) * 256]
                    b.mm(sb_, kps[:n, hd * 128:(hd + 1) * 128], gv3[:n, li, hd * 256:(hd + 1) * 256])
                for hd in range(4):
                    sb_ = bs[hd // 2][:, (hd % 2) * 256:(hd % 2 + 1) * 256]
                    ecol = eb[:, hd * n + hi - 1:hd * n + hi]
                    b.stt(S[:, hd * 256:(hd + 1) * 256], S[:, hd * 256:(hd + 1) * 256], ecol, sb_, ALU.mult, ALU.add)
                b.copy(Sb, S, "pool")

    masks128 = (cst[:, C_MINC:C_MINC + 128], cst[:, C_MSTL:C_MSTL + 128], cst[:, C_MC:C_MC + 128])
    masks16 = (cst[0:16, C_MINC16:C_MINC16 + 16], cst[0:16, C_MSTL16:C_MSTL16 + 16], cst[0:16, C_MC16:C_MC16 + 16])

    CAP = A.cap
    PBASE = (A.tops[0] + 63) // 64 * 64
    o_hT = PBASE
    o_kT = o_hT + KC * TT * 2
    o_vtok = o_kT + 8 * 2048 * 2
    end_kv = o_vtok + 16 * 1024 * 2
    o_qT = end_kv
    o_kTs = o_qT + 8 * TT * 2
    o_vs = o_kTs + 8 * TS * 2
    end_q = o_vs + 1024 * 2
    o_oTsb = end_q
    end_o = o_oTsb + 8 * TT * 2
    o_Sp = CAP - 6144
    o_oTgl = o_kT
    o_mixT = o_oTgl + 8 * TT * 2
    end_mix = o_mixT + KC * TT * 2
    o_gT = CAP - (NFF * TT * 2 + 1792 + 1408)
    assert end_o < o_Sp and end_mix <= o_oTsb and o_gT > o_kT

    A.set_ranges([(end_kv, o_Sp)])
    Sp = A.fixed("Sp", 1024, F32, o_Sp)
    Spb = A.fixed("Spb", 1024, BF16, o_Sp + 4096)
    kT = A.fixed("kT", 8 * 2048, BF16, o_kT)
    kT3 = kT.re("p (c n) -> p c n", c=8)
    vtok = A.fixed("vtok", 16 * 1024, BF16, o_vtok)
    vtok3 = vtok.re("p (t n) -> p t n", t=16)
    b.memset(Sp, 0.0, "pool")
    b.memset(Spb, 0.0, "pool")
    hTp = A.fixed("hTp", KC * TPRE, BF16, o_hT)
    hTp3 = hTp.re("p (k n) -> p k n", k=KC)
    A.mark()
    alloc_norm_work("a_")
    xbuf = [A.alloc(f"xbuf{i}", D) for i in range(3)]
    norm_pipeline([(128, ti * 128, PGROUP) for ti in range(NP_)],
                  lambda t, xt: b.dma("sp", xt, Tile(xp_d.ap[t * 128:(t + 1) * 128, :], None)),
                  S1, SH1, hTp3, "a_", xbuf)
    A.release()
    wb = [A.alloc(f"wb{i}", KC * 512, BF16) for i in range(2)]
    A.mark()
    alloc_qk_work("a_")
    kbf = [A.alloc(f"kbf{i}", 512, BF16) for i in range(3)]
    cnt = [0]
    pre_tiles = [(ti * 128, 128, ti) for ti in range(NP_)]

    def cons_kv_pre(cofs, cw, ti, n, bk):
        i = cnt[0]
        cnt[0] += 1
        if cofs < 1024:
            kb_ = kbf[i % 3]
            qk_post(bk, n, kg_b, None, kb_[:n, :], "a_")
            transpose_to(kT3, ti * 128, kb_, n, 4, (cofs // 512) * 4, "act")
        else:
            b.copy(vtok3[:n, ti, cofs - 1024:cofs - 1024 + cw], bk[:n, 0:cw], "act" if i % 2 else "dve")

    proj_tm(hTp3, pre_tiles, win_d, OFF_K, 2048, cons_kv_pre, wb, 512)
    A.release()
    A.mark()
    gla_pass("gp_", hTp3, TPRE,
             [dict(col0=ti * 128, n=128, masks=masks128, segs=[(0, 128, 0)]) for ti in range(NP_)],
             False, [(Sp, Spb)], wb, 512)
    A.release()
    b.ts(Sp, Sp, flg[:, 1:2], op0=ALU.mult)
    b.copy(Spb, Sp, "pool")

    A.set_ranges([(end_o, o_Sp)])
    hT = A.fixed("hT", KC * TT, BF16, o_hT)
    hT3 = hT.re("p (k n) -> p k n", k=KC)
    qT = A.fixed("qT", 8 * TT, BF16, o_qT)
    qT3 = qT.re("p (c n) -> p c n", c=8)
    kTs = A.fixed("kTs", 8 * TS, BF16, o_kTs)
    kTs3 = kTs.re("p (c n) -> p c n", c=8)
    vs_tok = A.fixed("vs_tok", 1024, BF16, o_vs, parts=16)
    oTsb = A.fixed("oTsb", 8 * TT, BF16, o_oTsb)
    oTsb3 = oTsb.re("p (c n) -> p c n", c=8)
    A.mark()
    alloc_norm_work("b_")
    xbuf = [A.alloc(f"xbufb{i}", D) for i in range(3)]

    def load_main(t, xt):
        if t < NT:
            b.dma("sp", xt, Tile(xm_d.ap[t * 128:(t + 1) * 128, :], None))
        else:
            b.dma("sp", xt[:TS, :], xs_d)

    norm_pipeline([(128, ti * 128, PGROUP) for ti in range(NT)] + [(TS, T, SGROUPS)],
                  load_main, S1, SH1, hT3, "b_", xbuf)
    A.release()
    A.mark()
    alloc_qk_work("b_")
    kbf = [A.alloc(f"kbfb{i}", 512, BF16) for i in range(3)]
    kf32 = [A.alloc(f"kf32b{i}", 512) for i in range(4)]
    wb = [A.alloc(f"wbb{i}", KC * 512, BF16) for i in range(2)]
    main_tiles = [(ti * 128, 128, ti) for ti in range(NT)] + [(T, TS, NT)]
    cnt[0] = 0

    def cons_qkv(cofs, cw, ti, n, bk):
        i = cnt[0]
        cnt[0] += 1
        is_s = (ti == NT)
        if cofs < 1024:
            qb = kbf[i % 3]
            qk_post(bk, n, qg_b, None, qb[:n, :], "b_")
            transpose_to(qT3, ti * 128, qb, n, 4, (cofs // 512) * 4, "act")
        elif cofs < 2048:
            kb_ = kbf[i % 3]
            kf = kf32[i % 4]
            qk_post(bk, n, kg_b, kf[:n, :], kb_[:n, :], "b_")
            cs = cofs - 1024
            if is_s:
                b.dma("sp", Tile(ks_d.ap[:, cs:cs + 512], ks_d.reg), kf[:n, :], is_output=True)
                transpose_to(kTs3, 0, kb_, n, 4, (cs // 512) * 4, "act")
            else:
                b.dma("sp", Tile(kp_d.ap[ti * 128:(ti + 1) * 128, cs:cs + 512], kp_d.reg), kf[:n, :], is_output=True)
                transpose_to(kT3, TPRE + ti * 128, kb_, n, 4, (cs // 512) * 4, "act")
        else:
            kf = kf32[i % 4]
            cs = cofs - 2048
            b.copy(kf[:n, :], bk[:n, :], "act")
            if is_s:
                b.dma("sp", Tile(vs_d.ap[:, cs:cs + 512], vs_d.reg), kf[:n, :], is_output=True)
                b.copy(vs_tok[:n, cs:cs + 512], kf[:n, :], "pool")
            else:
                b.dma("sp", Tile(vp_d.ap[ti * 128:(ti + 1) * 128, cs:cs + 512], vp_d.reg), kf[:n, :], is_output=True)
                b.copy(vtok3[:n, NP_ + ti, cs:cs + 512], kf[:n, :], "pool")

    proj_tm(hT3, main_tiles, win_d, OFF_Q, 3072, cons_qkv, wb, 512)
    A.release()

    A.mark()
    NQ = 384
    e2b = [A.alloc(f"e2b{i}", NQ) for i in range(2)]
    spb_ = [A.alloc(f"spb{i}", NQ, BF16) for i in range(3)]
    ab_ = [A.alloc(f"ab{i}", NQ, BF16) for i in range(3)]
    L32 = [A.alloc(f"L32{i}", NQ) for i in range(4)]
    L16 = [A.alloc(f"L16{i}", NQ, BF16) for i in range(4)]
    otok = A.alloc("otok", 3 * 1024, BF16)
    otok3 = otok.re("p (q n) -> p q n", q=3)
    zbanks = [banks[0], banks[1]]
    abanks = [banks[2], banks[3]]
    obanks = [banks[4], banks[5]]
    tbanks = [banks[6], banks[7]]
    units = []
    pr = 0
    for g in range(3):
        for c_ in range(NH // 2):
            kmax = NP_ + 3 * g + 2
            for kb in range(kmax, -1, -1):
                for hh in range(2):
                    units.append(dict(g=g, h=2 * c_ + hh, hh=hh, kb=kb, kmax=kmax, gh=2 * pr + hh, pr=pr,
                                      u=len(units)))
            pr += 1

    def stage_a(U):
        g, h, kb, u = U["g"], U["h"], U["kb"], U["u"]
        c, po = h // 2, (h % 2) * 64
        i = kb - (NP_ + 3 * g)
        off = max(i, 0) * 128
        N = NQ - off
        qlo = NQ * g + off
        l32, l16 = L32[U["gh"] % 4], L16[U["gh"] % 4]
        if kb == U["kmax"]:
            b.memset(l32, 0.0, "pool")
            b.memset(l16, 0.0, "pool")
        zb = zbanks[u % 2]
        e2 = e2b[u % 2]
        spb = spb_[u % 3]
        bias_t = (bias_pre if kb < NP_ else bias_all)[:, h:h + 1]
        U.update(off=off, N=N, qlo=qlo, c=c, po=po, i=i, bias_t=bias_t, spb=spb, l32=l32, l16=l16)
        b.mm(zb[:, off:NQ], kT3[po:po + 64, c, kb * 128:(kb + 1) * 128], qT3[po:po + 64, c, qlo:qlo + N])
        b.act(e2[:, off:NQ], zb[:, off:NQ], AF.Exp, scale=0.125, bias=bias_t)
        b.act(spb[:, off:NQ], e2[:, off:NQ], AF.Ln, scale=1.0, bias=1.0)
        if i >= 0:
            b.tt(spb[:, off:off + 128], spb[:, off:off + 128], mdiag_b, ALU.mult, eng="pool")

    def stage_b(U):
        h, kb, u, off, N, qlo, c, po, i = (U[x] for x in ("h", "kb", "u", "off", "N", "qlo", "c", "po", "i"))
        spb, l32, l16, bias_t = U["spb"], U["l32"], U["l16"], U["bias_t"]
        afb = abanks[u % 2]
        ab = ab_[u % 3]
        U["ab"] = ab
        firstblk = (kb == U["kmax"])
        b.mm(afb[:, off:NQ], tri8_b, spb[:, off:NQ], start=True, stop=False)
        if not firstblk:
            b.mm(afb[:, off:NQ], one8_b, l16[:, off:NQ], start=False, stop=False)
        b.mm(afb[:, off:NQ], kT3[po:po + 64, c, kb * 128:(kb + 1) * 128], qT3[po:po + 64, c, qlo:qlo + N],
             start=False, stop=True)
        b.act(ab[:, off:NQ], afb[:, off:NQ], AF.Exp, scale=0.125, bias=bias_t)
        if i >= 0:
            b.tt(ab[:, off:off + 128], ab[:, off:off + 128], mdiag_b, ALU.mult, eng="pool")
        if kb > 0:
            b.tt(l32[:, off:NQ], l32[:, off:NQ], spb[:, off:NQ], ALU.add, eng="pool")
            b.copy(l16[:, off:NQ], l32[:, off:NQ], "dve")

    def stage_c(U):
        g, h, kb, i, hh = U["g"], U["h"], U["kb"], U["i"], U["hh"]
        ob = obanks[U["pr"] % 2]
        ab = U["ab"]
        for qi in range(max(i, 0), 3):
            first = (kb == U["kmax"] and qi == max(i, 0) and hh == 0)
            b.mm(ob[:, hh * 256 + qi * 64:hh * 256 + (qi + 1) * 64], ab[:, qi * 128:(qi + 1) * 128],
                 vtok3[:, kb, h * 64:(h + 1) * 64], start=first, stop=(kb == 0), skip=True)
        if kb == 0:
            b.copy(otok3[:, :, h * 64:(h + 1) * 64],
                   ob[:, hh * 256:hh * 256 + 192].re("p (q d) -> p q d", q=3), "dve")
            if h == NH - 1:
                for qi in range(3):
                    transpose_to(oTsb3, (3 * g + qi) * 128, otok3[:, qi, :], 128, 8, 0, "dve", bank=tbanks[qi % 2])

    dm_w = [A.alloc(f"dm_w{i}", KC * 256, BF16) for i in range(2)]
    dm_b = [A.alloc(f"dm_b{i}", 256, F32, parts=5) for i in range(2)]
    dm_g = [A.alloc(f"dm_g{i}", 256, F32, parts=5) for i in range(2)]
    nu = len(units)
    dmi = 0
    dml = 0
    for step in range(nu + 2):
        if step % 19 == 2 and dml < len(deferred_mod):
            j_, c0_ = deferred_mod[dml]
            mod_load(j_, c0_, 256, dm_w[dml % 2], dm_b[dml % 2])
            dml += 1
        if step % 19 == 17 and dmi < dml:
            j_, c0_ = deferred_mod[dmi]
            mod_block(j_, c0_, 256, dm_w[dmi % 2], dm_b[dmi % 2], dm_g[dmi % 2], tbanks[1], load=False)
            dmi += 1
        if step < nu:
            stage_a(units[step])
        if 0 <= step - 1 < nu:
            stage_b(units[step - 1])
        if 0 <= step - 2 < nu:
            stage_c(units[step - 2])
    assert dmi == len(deferred_mod)
    A.release()

    A.set_ranges([(o_kT, end_kv), (end_o, o_Sp)])
    A.mark()
    NB = NPAGES + 1
    zall = A.alloc("zall", NB * 64)
    zall3 = zall.re("p (j n) -> p j n", j=NB)
    sps = A.alloc("sps", NB * 64)
    sps3 = sps.re("p (j n) -> p j n", j=NB)
    Ls = A.alloc("Ls", NB * 64)
    Ls3 = Ls.re("p (j n) -> p j n", j=NB)
    kpg = [A.alloc(f"kpg{i}", 1024, BF16) for i in range(4)]
    kTp = [A.alloc(f"kTp{i}", 1024, BF16) for i in range(2)]
    knew = A.alloc("knew", 1024, BF16)
    knew3 = knew.re("p (c n) -> p c n", c=8)
    spsb = A.alloc("spsb", NB * 64, BF16)
    Lsb = A.alloc("Lsb", NB * 64, BF16)
    aall = [A.alloc(f"aall{i}", NB * 64, BF16) for i in range(2)]
    vpg = [A.alloc(f"vpg{i}", 1024, BF16) for i in range(4)]
    vnew = A.alloc("vnew", 1024, BF16)
    t2s = A.alloc("t2s", 512)
    bsuf_t = A.alloc("bsuf", 512)
    ptb = t2s[:, 0:4 * NPAGES].bitcast(I32)
    ptf = t2s[:, 4 * NPAGES:8 * NPAGES]
    idxa = A.alloc("idxa", 4 * NPAGES, I32)
    qbd = A.alloc("qbd", 64, BF16)
    qbd3 = qbd.re("p (c n) -> p c n", c=8)
    b.dma("sp", ptb, Tile(pt_d.ap[0, :].partition_broadcast(128), None))
    b.copy(ptf, ptb, "dve")
    b.ts(ptf, ptf, 128.0, cst[:, C_IOTA:C_IOTA + 1], op0=ALU.mult, op1=ALU.add)
    b.copy(idxa, ptf, "dve")
    b.memset(knew, 0.0, "pool")
    b.memset(vnew, 0.0, "pool")
    b.memset(qbd, 0.0, "pool")
    zbank_s = [banks[0], banks[1]]
    trbank = [banks[2], banks[3]]
    obank_s = banks[4]
    afbank = [banks[5], banks[6]]
    pgc = [0]

    def s_kpass(sq_i):
        scol = T + 4 * sq_i
        b.copy(qbd3[0:64, :, 0:4], qT3[0:64, :, scol:scol + 4], "dve")
        b.copy(qbd3[64:128, :, 4:8], qT3[64:128, :, scol:scol + 4], "dve")
        b.copy(knew3[:, :, 0:4], kTs3[:, :, 4 * sq_i:4 * sq_i + 4], "dve")

        def qk(j, kt3):
            zb = zbank_s[(j // 8) % 2]
            zc = (j % 8) * 64
            for c in range(8):
                b.mm(zb[:, zc + c * 8:zc + c * 8 + 8], kt3[:, c, :], qbd3[:, c, :])
            if j % 8 == 7 or j == NB - 1:
                j0 = (j // 8) * 8
                nbk = j - j0 + 1
                b.stt(zall3[:, j0:j0 + nbk, :], zb[:, 0:nbk * 64].re("p (j n) -> p j n", j=nbk), 0.125,
                      bias_ht.un(1).bc([128, nbk, 64]), ALU.mult, ALU.add)

        pend = None
        for j in range(NPAGES):
            pg = pgc[0]
            pgc[0] += 1
            kp_ = kpg[pg % 4]
            ktp = kTp[pg % 2]
            tbk = trbank[pg % 2]
            b.gather(kp_, ck_d, idxa[:, sq_i * NPAGES + j:sq_i * NPAGES + j + 1])
            pb = tbk.bitcast(BF16)
            for c in range(8):
                b.tr(pb[:, c * 128:(c + 1) * 128], kp_[:, c * 128:(c + 1) * 128], ident_b)
            b.copy(ktp, pb, "act" if pg % 2 else "dve")
            if pend is not None:
                qk(*pend)
            pend = (j, ktp.re("p (c n) -> p c n", c=8))
        qk(*pend)
        qk(NB - 1, knew3)

    def s_post(sq_i):
        aa = aall[sq_i % 2]
        aa3 = aa.re("p (j n) -> p j n", j=NB)
        b.act(Ls, zall, AF.Exp)
        b.act(sps, Ls, AF.Ln, scale=1.0, bias=1.0)
        b.tt(sps3[:, NB - 1, :], sps3[:, NB - 1, :], m65_f, ALU.mult)
        b.copy(spsb, sps, "act")
        sp4 = sps[:, 0:NPAGES * 64].re("p (bt r n) -> p bt r n", bt=8, r=8)
        L4 = Ls[:, 0:NPAGES * 64].re("p (bt r n) -> p bt r n", bt=8, r=8)
        b.memset(Ls3[:, NB - 1, :], 0.0, "dve")
        b.memset(L4[:, :, 7, :], 0.0, "dve")
        for r in range(6, -1, -1):
            b.tt(L4[:, :, r, :], L4[:, :, r + 1, :], sp4[:, :, r + 1, :], ALU.add)
        btot = t2s.re("p (bt n) -> p bt n", bt=8)
        b.tt(btot, L4[:, :, 0, :], sp4[:, :, 0, :], ALU.add)
        bsuf = bsuf_t.re("p (bt n) -> p bt n", bt=8)
        b.copy(bsuf[:, 7, :], sps3[:, NB - 1, :], "dve")
        for bt in range(6, -1, -1):
            b.tt(bsuf[:, bt, :], bsuf[:, bt + 1, :], btot[:, bt + 1, :], ALU.add)
        b.tt(L4, L4, bsuf.un(2).bc([128, 8, 8, 64]), ALU.add)
        b.copy(Lsb, Ls, "act")
        b.tt(zall, zall, sps, ALU.subtract)
        for ch in range(0, NB * 64, 512):
            cwd = min(512, NB * 64 - ch)
            afb = afbank[(ch // 512) % 2]
            b.mm(afb[:, 0:cwd], triS_b, spsb[:, ch:ch + cwd], start=True, stop=False)
            b.mm(afb[:, 0:cwd], ones_b, Lsb[:, ch:ch + cwd], start=False, stop=True)
            b.tt(t2s[:, 0:cwd], zall[:, ch:ch + cwd], afb[:, 0:cwd], ALU.subtract)
            b.act(aa[:, ch:ch + cwd], t2s[:, 0:cwd], AF.Exp)
        b.tt(aa3[:, NB - 1, :], aa3[:, NB - 1, :], m65_b, ALU.mult)

    def s_vpass(sq_i):
        scol = T + 4 * sq_i
        aa3 = aall[sq_i % 2].re("p (j n) -> p j n", j=NB)
        b.dma("sp", vnew[0:4, :], vs_tok[4 * sq_i:4 * sq_i + 4, :])
        first = True
        for j in range(NB):
            if j < NPAGES:
                vp_ = vpg[j % 4]
                b.gather(vp_, cv_d, idxa[:, sq_i * NPAGES + j:sq_i * NPAGES + j + 1])
            else:
                vp_ = vnew
            for c in range(8):
                b.mm(obank_s[:, c * 64:(c + 1) * 64], vp_[:, c * 128:(c + 1) * 128], aa3[:, j, :],
                     start=first, stop=(j == NB - 1), skip=True)
                first = False
        for c in range(8):
            for hh in range(2):
                hcol = c * 64 + (2 * c + hh) * 4
                b.copy(oTsb3[hh * 64:(hh + 1) * 64, c, scol:scol + 4],
                       obank_s[hh * 64:(hh + 1) * 64, hcol:hcol + 4], "act" if hh else "dve")

    s_kpass(0)
    s_post(0)
    for sq_i in range(1, 4):
        s_kpass(sq_i)
        s_vpass(sq_i - 1)
        s_post(sq_i)
    s_vpass(3)
    A.release()

    A.set_ranges([(o_oTgl + 8 * TT * 2, o_oTsb), (end_o, o_Sp)])
    oTgl = A.fixed("oTgl", 8 * TT, BF16, o_oTgl)
    oTgl3 = oTgl.re("p (c n) -> p c n", c=8)
    A.mark()
    wb = [A.alloc(f"wbg{i}", KC * 256, BF16) for i in range(2)]
    for (t_lo, t_hi, with_s) in ((0, 3, False), (3, 6, False), (6, NT, True)):
        A.mark()
        states = [(Sp, Spb)]
        if with_s:
            Ss = []
            for i in range(4):
                S_ = A.alloc(f"Ss{i}", 1024)
                Sb_ = A.alloc(f"Ssb{i}", 1024, BF16)
                b.dma("sp", S_.re("p (h v) -> p h v", h=4), Tile(sg_d.ap[i].rearrange("h k v -> k h v"), None))
                b.copy(Sb_, S_, "pool")
                Ss.append((S_, Sb_))
            states = states + Ss
        cbase = t_lo * 128
        ntok = (t_hi - t_lo) * 128 + (TS if with_s else 0)
        hv = Tile(hT3.ap[:, :, cbase:cbase + ntok], hT3.reg)
        ov = Tile(oTgl3.ap[:, :, cbase:cbase + ntok], oTgl3.reg)
        tl = [dict(col0=(ti - t_lo) * 128, n=128, masks=masks128, segs=[(0, 128, 0)]) for ti in range(t_lo, t_hi)]
        if with_s:
            tl.append(dict(col0=(t_hi - t_lo) * 128, n=TS, masks=masks16,
                           segs=[(4 * i, 4 * i + 4, 1 + i) for i in range(4)]))
        gla_pass(f"gm{t_lo}_", hv, ntok, tl, True, states, wb, 256, oT3=ov)
        if with_s:
            b.dma("sp", Tile(sgp_d.ap.rearrange("h k v -> k h v"), sgp_d.reg), Sp.re("p (h v) -> p h v", h=4),
                  is_output=True)
            for i in range(4):
                b.dma("sp", Tile(sgs_d.ap[i].rearrange("h k v -> k h v"), sgs_d.reg),
                      Ss[i][0].re("p (h v) -> p h v", h=4), is_output=True)
        A.release()
    A.release()

    A.set_ranges([(end_mix, o_oTsb), (end_o, o_Sp)])
    mixT = A.fixed("mixT", KC * TT, BF16, o_mixT)
    mixT3 = mixT.re("p (k n) -> p k n", k=KC)
    A.mark()
    wsb = [A.alloc(f"wsb{i}", 8 * 256, BF16) for i in range(2)]
    wgl = [A.alloc(f"wgl{i}", 8 * 256, BF16) for i in range(2)]
    wms = [A.alloc(f"wms{i}", KC * 256, BF16) for i in range(2)]
    wmg = [A.alloc(f"wmg{i}", KC * 256, BF16) for i in range(2)]
    s1b = [A.alloc(f"s1b{i}", 512) for i in range(2)]
    s2b = [A.alloc(f"s2b{i}", 512) for i in range(2)]
    tbl = [(0, 512), (512, 512), (1024, TT - 1024)]
    it = 0
    for fg in range(8):
        Wsb = load_w_block(wsb[fg % 2], wbs_d, fg * 256, 256, nk=8)
        Wgl = load_w_block(wgl[fg % 2], wbg_d, fg * 256, 256, nk=8)
        Wms = load_w_block(wms[fg % 2], win_d, OFF_MSB + fg * 256, 256)
        Wmg = load_w_block(wmg[fg % 2], win_d, OFF_MGLA + fg * 256, 256)
        for f2 in range(2):
            fc = fg * 2 + f2
            cs = slice(f2 * 128, (f2 + 1) * 128)
            for (t0, tw) in tbl:
                bks = [banks[(it % 2) * 4 + k] for k in range(4)]
                s1, s2 = s1b[it % 2], s2b[it % 2]
                it += 1
                for kc in range(8):
                    b.mm(bks[0][:, 0:tw], Wsb[:, kc, cs], oTsb3[:, kc, t0:t0 + tw], start=(kc == 0), stop=(kc == 7))
                for kc in range(8):
                    b.mm(bks[1][:, 0:tw], Wgl[:, kc, cs], oTgl3[:, kc, t0:t0 + tw], start=(kc == 0), stop=(kc == 7))
                for kc in range(KC):
                    b.mm(bks[2][:, 0:tw], Wms[:, kc, cs], hT3[:, kc, t0:t0 + tw], start=(kc == 0), stop=(kc == KC - 1))
                for kc in range(KC):
                    b.mm(bks[3][:, 0:tw], Wmg[:, kc, cs], hT3[:, kc, t0:t0 + tw], start=(kc == 0), stop=(kc == KC - 1))
                b.act(s1[:, 0:tw], bks[2][:, 0:tw], AF.Sigmoid)
                b.act(s2[:, 0:tw], bks[3][:, 0:tw], AF.Sigmoid)
                b.tt(s1[:, 0:tw], s1[:, 0:tw], bks[0][:, 0:tw], ALU.mult)
                b.tt(s2[:, 0:tw], s2[:, 0:tw], bks[1][:, 0:tw], ALU.mult)
                b.tt(mixT3[:, fc, t0:t0 + tw], s1[:, 0:tw], s2[:, 0:tw], ALU.add)
    A.release()
    h2T3 = hT3

    A.set_ranges([(end_mix, CAP), (o_kT, o_mixT)])
    A.mark()
    woq = []
    for q in range(4):
        wq_ = A.alloc(f"wo{q}", KC * 512, BF16, rng=0)
        wq3 = wq_.re("p (k n) -> p k n", k=KC)
        b.dma("pool", wq3, Tile(wout_d.ap[:, q * 512:(q + 1) * 512].rearrange("(k p) n -> p k n", p=128), None))
        woq.append(wq3)
    xbuf = [A.alloc(f"xbufc{i}", D, rng=0) for i in range(3)]
    Gp = A.alloc("Gp", D, rng=0)
    Gs = A.alloc("Gs", D, F32, parts=16, rng=1)

    def load_gates(gi, Gp, Gs):
        b.dma("sp", Gp, Tile(gate_d.ap[gi, 0, :].partition_broadcast(128), gate_d.reg))
        for i in range(4):
            b.dma("sp", Gs[4 * i:4 * i + 4, :], Tile(gate_d.ap[gi, 1 + i, :].partition_broadcast(4), gate_d.reg))

    load_gates(0, Gp, Gs)
    alloc_norm_work("c_", rngs=(1, 1, 0))
    tq2 = [A.alloc("tq0", 512, rng=0)] * 2
    ytile_regs = [P.region(f"ytile{i}") for i in range(NT + 1)]

    def wo_mm(ti):
        n = 128 if ti < NT else TS
        c0 = ti * 128
        xt = xbuf[ti % 3]
        if ti < NT:
            b.dma("sp", xt, Tile(xm_d.ap[c0:c0 + 128, :], None))
        else:
            b.dma("sp", xt[:TS, :], xs_d)
        bks = []
        for q in range(4):
            bk = banks[(ti % 2) * 4 + q]
            for kc in range(KC):
                b.mm(bk[:n, :], mixT3[:, kc, c0:c0 + n], woq[q][:, kc, :],
                     start=(kc == 0), stop=(kc == KC - 1))
            bks.append(bk)
        return bks

    def wo_post(ti, bks):
        n = 128 if ti < NT else TS
        c0 = ti * 128
        xt = xbuf[ti % 3]
        G = Gp if ti < NT else Gs
        for q in range(4):
            tq = tq2[q % 2]
            b.tt(tq[:n, :], bks[q][:n, :], G[:n, q * 512:(q + 1) * 512], ALU.mult)
            b.tt(xt[:n, q * 512:(q + 1) * 512], xt[:n, q * 512:(q + 1) * 512], tq[:n, :], ALU.add)
        if ti < NT:
            b.dma("sp", Tile(y_d.ap[c0:c0 + 128, :], ytile_regs[ti]), xt)
        else:
            b.dma("sp", Tile(ys_d.ap, ytile_regs[ti]), xt[:TS, :])
        norm_stats(xt, n, "c_", ti)

    def wo_tr(ti, bks):
        n = 128 if ti < NT else TS
        norm_transpose(n, S2, SH2, h2T3, ti * 128, PGROUP if ti < NT else SGROUPS, "c_", ti,
                       bankpair=(bks[0], bks[1]))

    bks_of = {0: wo_mm(0)}
    for ti in range(NT + 2):
        if ti - 1 >= 0:
            wo_tr(ti - 1, bks_of[ti - 1])
        if ti + 1 <= NT:
            bks_of[ti + 1] = wo_mm(ti + 1)
        if ti <= NT:
            wo_post(ti, bks_of[ti])
    A.release()

    A.set_ranges([(o_kT, o_gT)])
    gT = A.fixed("gT", NFF * TT, BF16, o_gT)
    gT3 = gT.re("p (j n) -> p j n", j=NFF)
    cvT = A.fixed("cvT", NFF * 10, F32, o_gT + NFF * TT * 2)
    cvT3 = cvT.re("p (j n) -> p j n", j=NFF)
    scT_ = A.fixed("scT_", NFF * 8, F32, o_gT + NFF * TT * 2 + 1792)
    scT3_ = scT_.re("p (j n) -> p j n", j=NFF)
    A.mark()
    scrow = A.alloc("scrow", DFF, F32, parts=8)
    b.dma("sp", scrow, sc_d)
    for j0 in range(0, NFF, 8):
        nj = min(8, NFF - j0)
        bk = next_bank()
        for jj in range(nj):
            b.tr(bk[:, jj * 8:jj * 8 + 8], scrow[0:8, (j0 + jj) * 128:(j0 + jj + 1) * 128], ident_f[0:8, 0:8])
        b.copy(scT3_[:, j0:j0 + nj, :], bk[:, 0:nj * 8].re("p (j n) -> p j n", j=nj), "dve")
    A.release()
    A.mark()
    wug = [A.alloc(f"wug{i}", KC * 256, BF16) for i in range(2)]
    wuv = [A.alloc(f"wuv{i}", KC * 256, BF16) for i in range(2)]
    abuf = [A.alloc(f"abuf{i}", 2 + T) for i in range(2)]
    asb = [A.alloc(f"asb{i}", 24) for i in range(2)]
    cbuf = [A.alloc(f"cbuf{i}", T) for i in range(2)]
    csb = [A.alloc(f"csb{i}", 16) for i in range(2)]
    for i in range(2):
        b.memset(abuf[i][:, 0:2], 0.0, "pool")
    Wg_n = load_w_block(wug[0], wup_d, 0, 256)
    Wv_n = load_w_block(wuv[0], wup_d, DFF, 256)
    for jg in range(NFF // 2):
        Wg, Wv = Wg_n, Wv_n
        if jg + 1 < NFF // 2:
            Wg_n = load_w_block(wug[(jg + 1) % 2], wup_d, (jg + 1) * 256, 256)
            Wv_n = load_w_block(wuv[(jg + 1) % 2], wup_d, DFF + (jg + 1) * 256, 256)
        for j2 in range(2):
            j = jg * 2 + j2
            cs = slice(j2 * 128, (j2 + 1) * 128)
            ab, asq, cb_, csq = abuf[j % 2], asb[j % 2], cbuf[j % 2], csb[j % 2]
            as3 = asq.re("p (s n) -> p s n", s=4)
            cs3 = csq.re("p (s n) -> p s n", s=4)
            w0 = cwT[:, j:j + 1]
            w1 = cwT[:, 44 + j:45 + j]
            w2 = cwT[:, 88 + j:89 + j]
            cbias = cwT[:, 132 + j:133 + j]
            vbk = []
            for (t0, tw) in tbl:
                bg = next_bank()
                for kc in range(KC):
                    b.mm(bg[:, 0:tw], Wg[:, kc, cs], h2T3[:, kc, t0:t0 + tw], start=(kc == 0), stop=(kc == KC - 1))
                if t0 < 1024:
                    b.copy(ab[:, 2 + t0:2 + t0 + tw], bg[:, 0:tw], "act")
                else:
                    b.copy(ab[:, 2 + 1024:2 + T], bg[:, 0:T - 1024], "act")
                    b.copy(as3[:, :, 0:2], scT3_[:, j, :].re("p (s n) -> p s n", s=4), "pool")
                    b.copy(as3[:, :, 2:6], bg[:, T - 1024:TT - 1024].re("p (s n) -> p s n", s=4), "act")
            for (t0, tw) in tbl:
                bv = next_bank()
                for kc in range(KC):
                    b.mm(bv[:, 0:tw], Wv[:, kc, cs], h2T3[:, kc, t0:t0 + tw], start=(kc == 0), stop=(kc == KC - 1))
                vbk.append(bv)
            b.ts(cb_, ab[:, 2:2 + T], w2, cbias, op0=ALU.mult, op1=ALU.add)
            b.stt(cb_, ab[:, 1:1 + T], w1, cb_, ALU.mult, ALU.add)
            b.stt(cb_, ab[:, 0:T], w0, cb_, ALU.mult, ALU.add)
            b.act(cb_, cb_, AF.Silu)
            b.ts(cs3, as3[:, :, 2:6], w2, cbias, op0=ALU.mult, op1=ALU.add)
            b.stt(cs3, as3[:, :, 1:5], w1, cs3, ALU.mult, ALU.add)
            b.stt(cs3, as3[:, :, 0:4], w0, cs3, ALU.mult, ALU.add)
            b.act(csq, csq, AF.Silu)
            b.copy(cvT3[:, j, 0:2], ab[:, T:T + 2], "pool")
            b.copy(cvT3[:, j, 2:10].re("p (s n) -> p s n", s=4), as3[:, :, 4:6], "pool")
            for bi_, (t0, tw) in enumerate(tbl):
                if t0 < 1024:
                    b.tt(gT3[:, j, t0:t0 + tw], cb_[:, t0:t0 + tw], vbk[bi_][:, 0:tw], ALU.mult)
                else:
                    b.tt(gT3[:, j, 1024:T], cb_[:, 1024:T], vbk[bi_][:, 0:T - 1024], ALU.mult)
                    b.tt(gT3[:, j, T:TT], csq, vbk[bi_][:, T - 1024:TT - 1024], ALU.mult)
    A.release()
    A.set_ranges([(PBASE, o_gT)])
    A.mark()
    cvrow = A.alloc("cvrow", DFF, F32, parts=16)
    for j0 in range(0, NFF, 4):
        bk = next_bank()
        for jj in range(4):
            b.tr(bk[0:10, jj * 128:(jj + 1) * 128], cvT3[:, j0 + jj, :], ident_f)
        b.copy(cvrow[0:10, j0 * 128:(j0 + 4) * 128], bk[0:10, :], "act")
    b.dma("sp", cvp_d, cvrow[0:2, :], is_output=True)
    b.dma("sp", cvs_d, cvrow[2:10, :], is_output=True)
    A.release()

    A.mark()
    Gp = A.alloc("Gp2", D)
    Gs = A.alloc("Gs2", D, F32, parts=16)
    load_gates(1, Gp, Gs)
    HJ = NFF // 2
    wd = [A.alloc(f"wd{i}", HJ * 512, BF16) for i in range(2)]
    xmb = [A.alloc(f"xmb{i}", 512) for i in range(4)]
    ob_ = [A.alloc(f"ob{i}", 512) for i in range(4)]
    xi = 0
    Wh_n = [load_w_block(wd[half], wdn_d, 0, 512, nk=HJ, r0=half * HJ * 128) for half in range(2)]
    for sl in range(4):
        c_lo = sl * 512
        Wh = list(Wh_n)
        for pi, tiles_ in enumerate((list(range(8)), [8, 9])):
            for half in range(2):
                for jj in range(HJ):
                    j = half * HJ + jj
                    for bi_, ti in enumerate(tiles_):
                        n = 128 if ti < NT else TS
                        c0 = ti * 128
                        b.mm(banks[bi_][:n, :], gT3[:, j, c0:c0 + n], Wh[half][:, jj, :],
                             start=(j == 0), stop=(j == NFF - 1))
                if pi == 1 and sl + 1 < 4:
                    Wh_n[half] = load_w_block(wd[half], wdn_d, c_lo + 512, 512, nk=HJ, r0=half * HJ * 128)
            for bi_, ti in enumerate(tiles_):
                n = 128 if ti < NT else TS
                c0 = ti * 128
                G = Gp if ti < NT else Gs
                xm_ = xmb[xi % 4]
                o_ = ob_[xi % 4]
                xi += 1
                if ti < NT:
                    src = Tile(y_d.ap[c0:c0 + 128, c_lo:c_lo + 512], ytile_regs[ti])
                else:
                    src = Tile(ys_d.ap[:, c_lo:c_lo + 512], ytile_regs[ti])
                b.dma("sp", xm_[:n, :], src)
                b.tt(o_[:n, :], banks[bi_][:n, :], G[:n, c_lo:c_lo + 512], ALU.mult)
                b.tt(o_[:n, :], o_[:n, :], xm_[:n, :], ALU.add)
                b.dma("sp", src, o_[:n, :], is_output=True)
    A.release()

    P.emit()
    return nc, P, A


_CACHE = {}


def kernel(x_prompt, x_sample, c_prompt, c_sample, cache_k_sb, cache_v_sb, page_table, state_gla, state_ffn_conv,
           norm1_g, norm2_g, w_mod, b_mod, w_in, q_norm_g, k_norm_g, sb_bias, w_gk_up, b_gk, gla_onorm_g,
           w_br_sb, w_br_gla, w_out, w_up, conv_w, conv_b, w_down):
    f = lambda a: np.ascontiguousarray(np.asarray(a, dtype=np.float32))
    if "nc" not in _CACHE:
        _CACHE["nc"] = build_program()[0]
    nc = _CACHE["nc"]
    x_prompt = f(x_prompt)
    x_sample = f(x_sample)
    ck = f(cache_k_sb)[0].reshape(NPOOL_ROWS, 1024)
    cv = f(cache_v_sb)[0].reshape(NPOOL_ROWS, 1024)
    cst = make_consts()
    shared = {
        "ck": ck, "cv": cv, "cst": cst,
        "norm1_g": f(norm1_g)[0].reshape(16, 128), "norm2_g": f(norm2_g)[0].reshape(16, 128),
        "w_mod": f(w_mod)[0], "b_mod": f(b_mod)[0].reshape(96, 128), "b_mod_row": f(b_mod)[0].reshape(1, 6 * D),
        "w_in": f(w_in)[0], "q_norm_g": f(q_norm_g)[0].reshape(1, HD), "k_norm_g": f(k_norm_g)[0].reshape(1, HD),
        "sb_bias": f(sb_bias)[0].reshape(1, NH), "w_gk_up": f(w_gk_up)[0], "b_gk": f(b_gk)[0].reshape(1, 512),
        "gla_onorm_g": f(gla_onorm_g)[0].reshape(1, 256), "w_br_sb": f(w_br_sb)[0], "w_br_gla": f(w_br_gla)[0],
        "w_out": f(w_out)[0], "w_up": f(w_up)[0], "conv_w": f(conv_w)[0].reshape(132, 128),
        "conv_b": f(conv_b)[0].reshape(44, 128), "w_down": f(w_down)[0],
    }
    pt = np.asarray(page_table).astype(np.int32)
    in_maps = []
    for core in range(8):
        bq, half = core // 2, core % 2
        lo = 0 if half == 0 else 2048 - T
        flg = np.zeros((128, 2), np.float32)
        flg[:, 0] = PENALTY if half == 0 else 0.0
        flg[:, 1] = 0.0 if half == 0 else 1.0
        sl = slice(4 * core, 4 * core + 4)
        ccT = np.concatenate([f(c_prompt)[bq:bq + 1], f(c_sample)[sl]], axis=0).T
        m = dict(shared)
        m.update({
            "xm": np.ascontiguousarray(x_prompt[bq, lo:lo + T]),
            "xp": np.ascontiguousarray(x_prompt[bq, 0:TPRE]),
            "xs": np.ascontiguousarray(x_sample[sl].reshape(TS, D)),
            "flg": flg,
            "ccT": np.ascontiguousarray(ccT),
            "pt": np.ascontiguousarray(pt[sl].reshape(1, 4 * NPAGES)),
            "sg": np.ascontiguousarray(f(state_gla)[0, sl]),
            "sc": np.ascontiguousarray(f(state_ffn_conv)[0, sl].reshape(8, DFF)),
        })
        in_maps.append(m)
    res = run_bass_kernel_spmd(nc, in_maps, core_ids=list(range(8)))
    R = res.results
    y_prompt = np.zeros((4, 2048, D), np.float32)
    k_prompt = np.zeros((1, 4, 2048, NH, HD), np.float32)
    v_prompt = np.zeros((1, 4, 2048, NH, HD), np.float32)
    gla_p = np.zeros((1, 4, 4, 128, 256), np.float32)
    conv_p = np.zeros((1, 4, 2, DFF), np.float32)
    y_sample = np.zeros((32, 4, D), np.float32)
    k_sample = np.zeros((1, 32, 4, NH, HD), np.float32)
    v_sample = np.zeros((1, 32, 4, NH, HD), np.float32)
    gla_s = np.zeros((1, 32, 4, 128, 256), np.float32)
    conv_s = np.zeros((1, 32, 2, DFF), np.float32)
    for core in range(8):
        bq, half = core // 2, core % 2
        r = R[core]
        if half == 0:
            rows, dst = slice(0, 1024), slice(0, 1024)
        else:
            rows, dst = slice(T - 1024, T), slice(1024, 2048)
            gla_p[0, bq] = r["sgp"]
            conv_p[0, bq] = r["cvp"]
        y_prompt[bq, dst] = r["y"][rows]
        k_prompt[0, bq, dst] = r["kp"][rows].reshape(1024, NH, HD)
        v_prompt[0, bq, dst] = r["vp"][rows].reshape(1024, NH, HD)
        sl = slice(4 * core, 4 * core + 4)
        y_sample[sl] = r["ys"].reshape(4, 4, D)
        k_sample[0, sl] = r["ks"].reshape(4, 4, NH, HD)
        v_sample[0, sl] = r["vs"].reshape(4, 4, NH, HD)
        gla_s[0, sl] = r["sgs"]
        conv_s[0, sl] = r["cvs"].reshape(4, 2, DFF)
    return (y_prompt, y_sample, k_prompt, v_prompt, gla_p, conv_p, k_sample, v_sample, gla_s, conv_s)
```

```python
import numpy as np
import concourse.bass as bass
import concourse.mybir as mybir
from concourse.bass_utils import run_bass_kernel_spmd

F32 = mybir.dt.float32
BF16 = mybir.dt.bfloat16
I32 = mybir.dt.int32
AF = mybir.ActivationFunctionType
ALU = mybir.AluOpType
AX = mybir.AxisListType

ENGS = ("sp", "act", "dve", "pool", "pe")


class Op:
    __slots__ = ("eng", "fn", "deps", "marked", "sem", "val", "is_dma")

    def __init__(self, eng, fn, is_dma):
        self.eng = eng
        self.fn = fn
        self.deps = []
        self.marked = False
        self.sem = None
        self.val = 0
        self.is_dma = is_dma


class Region:
    __slots__ = ("name", "last_w", "readers", "dma_readers", "excl", "fence")

    def __init__(self, name, excl=False, fence=None):
        self.name = name
        self.fence = fence
        self.last_w = None
        self.readers = {}
        self.dma_readers = []
        self.excl = excl


class Prog:
    def __init__(self, nc, n_dma_sems=(24, 4, 24)):
        self.nc = nc
        self.ops = {e: [] for e in ENGS}
        self.eng_sem = {e: nc.alloc_semaphore("s_" + e) for e in ENGS}
        self.dma_sems = {
            "sp": [nc.alloc_semaphore(f"d_sp{i}") for i in range(n_dma_sems[0])],
            "act": [nc.alloc_semaphore(f"d_act{i}") for i in range(n_dma_sems[1])],
            "pool": [nc.alloc_semaphore(f"d_pool{i}") for i in range(n_dma_sems[2])],
        }
        self.dma_cnt = {q: [0] * len(v) for q, v in self.dma_sems.items()}
        self.dma_rr = {q: 0 for q in self.dma_sems}
        self.out_dmas = []
        self.n_ops = 0

    def region(self, name, excl=False, fence=None):
        return Region(name, excl, fence)

    def make_fence(self):
        f = []
        for e in ENGS:
            for op in reversed(self.ops[e]):
                if not op.is_dma:
                    f.append(op)
                    break
        latest = {}
        for q in self.dma_sems:
            for op in self.ops[q]:
                if op.is_dma:
                    latest[id(op.sem)] = op
        f.extend(latest.values())
        return f

    def _add(self, op, reads, writes):
        deps = []
        eng = op.eng
        wr = list(writes)
        for r in list(reads) + wr:
            if r.fence is not None:
                deps.extend(r.fence)
                r.fence = None
        for r in reads:
            if r.excl:
                wr.append(r)
                continue
            lw = r.last_w
            if lw is not None:
                if lw.is_dma or lw.eng != eng or eng != "pe":
                    deps.append(lw)
            if op.is_dma:
                r.dma_readers.append(op)
            else:
                r.readers[eng] = op
        for w in wr:
            lw = w.last_w
            if lw is not None:
                if lw.is_dma or op.is_dma or lw.eng != eng:
                    deps.append(lw)
                elif w.excl and eng != "pe":
                    deps.append(lw)
            for e2, ro in w.readers.items():
                if op.is_dma or e2 != eng:
                    deps.append(ro)
            deps.extend(w.dma_readers)
            w.last_w = op
            w.readers = {}
            w.dma_readers = []
        seen = set()
        for d in deps:
            if d is op or id(d) in seen:
                continue
            seen.add(id(d))
            op.deps.append(d)
            d.marked = True
        self.ops[eng].append(op)
        self.n_ops += 1
        return op

    def op(self, eng, fn, reads=(), writes=()):
        return self._add(Op(eng, fn, False), reads, writes)

    def dma(self, queue, fn, reads=(), writes=(), is_output=False):
        op = Op(queue, fn, True)
        k = self.dma_rr[queue]
        self.dma_rr[queue] = (k + 1) % len(self.dma_sems[queue])
        self.dma_cnt[queue][k] += 1
        op.sem = self.dma_sems[queue][k]
        op.val = 16 * self.dma_cnt[queue][k]
        op.marked = True
        self._add(op, reads, writes)
        if is_output:
            self.out_dmas.append(op)
        return op

    def emit(self):
        nc = self.nc
        fin = Op("sp", None, False)
        fin.deps = list(self.out_dmas)
        self.ops["sp"].append(fin)
        for e in ENGS:
            cnt = 0
            for op in self.ops[e]:
                if op.is_dma:
                    continue
                if op.marked:
                    cnt += 1
                    op.sem = self.eng_sem[e]
                    op.val = cnt

        def run(ename, eng):
            waited = {}
            for op in self.ops[ename]:
                for d in op.deps:
                    key = id(d.sem)
                    if waited.get(key, 0) >= d.val:
                        continue
                    waited[key] = d.val
                    eng.wait_ge(d.sem, d.val)
                if op.fn is None:
                    continue
                ins = op.fn(eng)
                if op.marked:
                    ins.then_inc(op.sem, 16 if op.is_dma else 1)

        with nc.Block() as block:
            @block.sync
            def _(e):
                run("sp", e)

            @block.scalar
            def _(e):
                run("act", e)

            @block.vector
            def _(e):
                run("dve", e)

            @block.gpsimd
            def _(e):
                run("pool", e)

            @block.tensor
            def _(e):
                run("pe", e)


class Tile:
    __slots__ = ("ap", "reg")

    def __init__(self, ap, reg):
        self.ap = ap
        self.reg = reg

    def __getitem__(self, key):
        return Tile(self.ap[key], self.reg)

    def re(self, pattern_, **kw):
        return Tile(self.ap.rearrange(pattern_, **kw), self.reg)

    def bc(self, shape):
        return Tile(self.ap.to_broadcast(list(shape)), self.reg)

    def un(self, axis):
        return Tile(self.ap.unsqueeze(axis), self.reg)

    def bitcast(self, dt):
        return Tile(self.ap.bitcast(dt), self.reg)


class Arena:
    def __init__(self, nc, prog, nbytes):
        self.prog = prog
        self.t = nc.alloc_sbuf_tensor("arena", [128, nbytes // 4], F32)
        self.ap = self.t.ap()
        self.cap = nbytes
        self.ranges = [(0, nbytes)]
        self.tops = [0]
        self.marks = []
        self.fence = None

    def set_ranges(self, ranges):
        self.ranges = [tuple(r) for r in ranges]
        self.tops = [r[0] for r in ranges]
        self.marks = []
        self.fence = self.prog.make_fence()

    def mark(self):
        self.marks.append(list(self.tops))

    def release(self):
        self.tops = self.marks.pop()
        self.fence = self.prog.make_fence()

    def _tile(self, name, off, nb, cols, dtype, parts):
        assert off % 4 == 0 and off + nb <= self.cap
        a = self.ap[:, off // 4:(off + nb) // 4]
        if dtype != F32:
            a = a.bitcast(dtype)
        return Tile(a[:parts, :cols], self.prog.region(name, fence=self.fence))

    def alloc(self, name, cols, dtype=F32, parts=128, rng=None):
        esz = 2 if dtype == BF16 else 4
        nb = (cols * esz + 63) // 64 * 64
        for i, (lo, hi) in enumerate(self.ranges):
            if rng is not None and i != rng:
                continue
            if self.tops[i] + nb <= hi:
                off = self.tops[i]
                self.tops[i] += nb
                return self._tile(name, off, nb, cols, dtype, parts)
        raise AssertionError(f"SBUF overflow at {name} ({nb} B): ranges {self.ranges} tops {self.tops}")

    def fixed(self, name, cols, dtype, off, parts=128):
        esz = 2 if dtype == BF16 else 4
        nb = (cols * esz + 63) // 64 * 64
        return self._tile(name, off, nb, cols, dtype, parts)


def _regs(ts):
    return [t.reg for t in ts if isinstance(t, Tile) and t.reg is not None]


class B:
    def __init__(self, P):
        self.P = P

    def dma(self, q, out, in_, is_output=False):
        o, i = out.ap, in_.ap
        if q == "pool_ind":
            raise ValueError
        return self.P.dma(q, lambda e: e.dma_start(out=o, in_=i), _regs([in_]), _regs([out]), is_output)

    def gather(self, out, table, idx):
        o, t, ix = out.ap, table.ap, idx.ap
        return self.P.dma(
            "pool",
            lambda e: e.indirect_dma_start(out=o, out_offset=None, in_=t,
                                           in_offset=bass.IndirectOffsetOnAxis(ap=ix, axis=0)),
            _regs([idx, table]), _regs([out]))

    def act(self, out, in_, func, bias=None, scale=None, accum=None):
        kw = {}
        rd = [in_]
        wr = [out]
        if bias is not None:
            kw["bias"] = bias.ap if isinstance(bias, Tile) else bias
            rd.append(bias)
        if scale is not None:
            kw["scale"] = scale.ap if isinstance(scale, Tile) else scale
            rd.append(scale)
        if accum is not None:
            kw["accum_out"] = accum.ap
            wr.append(accum)
            rd.append(accum)
        o, i = out.ap, in_.ap
        return self.P.op("act", lambda e: e.activation(out=o, in_=i, func=func, **kw), _regs(rd), _regs(wr))

    def ts(self, out, in0, s1, s2=None, op0=ALU.mult, op1=None, eng="dve"):
        a1 = s1.ap if isinstance(s1, Tile) else s1
        a2 = s2.ap if isinstance(s2, Tile) else s2
        o, i = out.ap, in0.ap
        if op1 is None:
            fn = lambda e: e.tensor_scalar(out=o, in0=i, scalar1=a1, scalar2=None, op0=op0)
        else:
            fn = lambda e: e.tensor_scalar(out=o, in0=i, scalar1=a1, scalar2=a2, op0=op0, op1=op1)
        return self.P.op(eng, fn, _regs([in0, s1, s2]), _regs([out]))

    def tt(self, out, in0, in1, op, eng="dve"):
        o, a, b = out.ap, in0.ap, in1.ap
        return self.P.op(eng, lambda e: e.tensor_tensor(out=o, in0=a, in1=b, op=op), _regs([in0, in1]), _regs([out]))

    def stt(self, out, in0, scalar, in1, op0, op1):
        o, a, b = out.ap, in0.ap, in1.ap
        s = scalar.ap if isinstance(scalar, Tile) else scalar
        return self.P.op("dve", lambda e: e.scalar_tensor_tensor(out=o, in0=a, scalar=s, in1=b, op0=op0, op1=op1),
                         _regs([in0, in1, scalar]), _regs([out]))

    def copy(self, out, in_, eng="dve"):
        o, i = out.ap, in_.ap
        if eng == "act":
            return self.P.op("act", lambda e: e.activation(out=o, in_=i, func=AF.Copy), _regs([in_]), _regs([out]))
        return self.P.op(eng, lambda e: e.tensor_copy(out=o, in_=i), _regs([in_]), _regs([out]))

    def memset(self, t, val, eng="pool"):
        o = t.ap
        return self.P.op(eng, lambda e: e.memset(o, val), [], _regs([t]))

    def reduce(self, out, in_, op=ALU.add, eng="dve"):
        o, i = out.ap, in_.ap
        return self.P.op(eng, lambda e: e.tensor_reduce(out=o, in_=i, axis=AX.X, op=op), _regs([in_]), _regs([out]))

    def mm(self, out, lhsT, rhs, start=True, stop=True, skip=False):
        o, l, r = out.ap, lhsT.ap, rhs.ap
        return self.P.op("pe", lambda e: e.matmul(o, lhsT=l, rhs=r, start=start, stop=stop, skip_group_check=skip),
                         _regs([lhsT, rhs]), _regs([out]))

    def tr(self, out, in_, ident):
        o, i, d = out.ap, in_.ap, ident.ap
        return self.P.op("pe", lambda e: e.transpose(out=o, in_=i, identity=d), _regs([in_, ident]), _regs([out]))


D = 2048
KC = 16
NT = 9
NP_ = 7
T = NT * 128
TPRE = NP_ * 128
TS = 16
TT = T + TS
NH = 16
HD = 64
DFF = 5632
NFF = DFF // 128
N_IN = 10256
OFF_Q, OFF_K, OFF_V = 0, 1024, 2048
OFF_GQ, OFF_GK, OFF_GV, OFF_GR, OFF_GD = 3072, 3584, 4096, 5120, 6144
OFF_MSB, OFF_MGLA = 6160, 8208
NPAGES = 64
NPOOL_ROWS = 2560 * 128
EPS = 1e-6
PENALTY = -30000.0

C_ID, C_TRIS, C_ONES, C_MDIAG, C_MINC, C_MSTL, C_MC = 0, 128, 256, 384, 512, 640, 768
C_IOTA = 896
C_MINC16, C_MSTL16, C_MC16 = 897, 913, 929
C_SEGC = 945
C_SEGR = 1009
C_M65 = 1013
C_TRI8 = 1077
C_ONE8 = 1205
C_TOT = 1333


def make_consts():
    c = np.zeros((128, C_TOT), np.float32)
    s = np.arange(128)[:, None]
    t = np.arange(128)[None, :]
    c[:, C_ID:C_ID + 128] = (s == t)
    c[:, C_TRIS:C_TRIS + 128] = (s > t)
    c[:, C_ONES:C_ONES + 128] = 1.0
    c[:, C_MDIAG:C_MDIAG + 128] = (s < t)
    c[:, C_MINC:C_MINC + 128] = (s <= t) * (-1.0 / 16.0)
    c[:, C_MSTL:C_MSTL + 128] = (s > t) * (-1.0 / 16.0)
    c[:, C_MC:C_MC + 128] = (s <= t)
    c[:, C_IOTA] = np.arange(128)
    s16 = np.arange(16)[:, None]
    t16 = np.arange(16)[None, :]
    same = (s16 // 4) == (t16 // 4)
    c[:16, C_MINC16:C_MINC16 + 16] = (same & (s16 <= t16)) * (-1.0 / 16.0)
    c[:16, C_MSTL16:C_MSTL16 + 16] = (same & (s16 > t16)) * (-1.0 / 16.0)
    c[:16, C_MC16:C_MC16 + 16] = (same & (s16 <= t16))
    for i in range(4):
        c[:, C_SEGC + i * 16 + 4 * i:C_SEGC + i * 16 + 4 * i + 4] = 1.0
        c[4 * i:4 * i + 4, C_SEGR + i] = 1.0
    for col in range(64):
        tq = col % 4
        c[:tq, C_M65 + col] = 1.0
    c[:, C_TRI8:C_TRI8 + 128] = (s >= t) * (-8.0)
    c[:, C_ONE8:C_ONE8 + 128] = -8.0
    return c


def build_program(stop_after=None):
    nc = bass.Bass("TRN2", target_bir_lowering=False)
    P = Prog(nc)
    b = B(P)
    A = Arena(nc, P, 206 * 1024)

    def din(name, shape, dt=F32):
        return Tile(nc.dram_tensor(name, list(shape), dt, kind="ExternalInput").ap(), None)

    def dout(name, shape, dt=F32):
        return Tile(nc.dram_tensor(name, list(shape), dt, kind="ExternalOutput").ap(), P.region(name))

    def dscr(name, shape, dt=F32):
        return Tile(nc.dram_tensor(name, list(shape), dt, kind="Internal").ap(), P.region(name))

    xm_d = din("xm", [T, D])
    xp_d = din("xp", [TPRE, D])
    xs_d = din("xs", [TS, D])
    flg_d = din("flg", [128, 2])
    ccT_d = din("ccT", [D, 5])
    pt_d = din("pt", [1, 4 * NPAGES], I32)
    ck_d = din("ck", [NPOOL_ROWS, 1024])
    cv_d = din("cv", [NPOOL_ROWS, 1024])
    sg_d = din("sg", [4, 4, 128, 256])
    sc_d = din("sc", [8, DFF])
    cst_d = din("cst", [128, C_TOT])
    n1g_d = din("norm1_g", [16, 128])
    n2g_d = din("norm2_g", [16, 128])
    wmod_d = din("w_mod", [D, 6 * D])
    bmod_d = din("b_mod", [96, 128])
    bmodr_d = din("b_mod_row", [1, 6 * D])
    win_d = din("w_in", [D, N_IN])
    qng_d = din("q_norm_g", [1, HD])
    kng_d = din("k_norm_g", [1, HD])
    sbb_d = din("sb_bias", [1, NH])
    wgk_d = din("w_gk_up", [16, 512])
    bgk_d = din("b_gk", [1, 512])
    gon_d = din("gla_onorm_g", [1, 256])
    wbs_d = din("w_br_sb", [1024, D])
    wbg_d = din("w_br_gla", [1024, D])
    wout_d = din("w_out", [D, D])
    wup_d = din("w_up", [D, 2 * DFF])
    cw_d = din("conv_w", [132, 128])
    cb_d = din("conv_b", [44, 128])
    wdn_d = din("w_down", [DFF, D])

    y_d = dout("y", [T, D])
    ys_d = dout("ys", [TS, D])
    kp_d = dout("kp", [T, 1024])
    vp_d = dout("vp", [T, 1024])
    sgp_d = dout("sgp", [4, 128, 256])
    cvp_d = dout("cvp", [2, DFF])
    ks_d = dout("ks", [TS, 1024])
    vs_d = dout("vs", [TS, 1024])
    sgs_d = dout("sgs", [4, 4, 128, 256])
    cvs_d = dout("cvs", [8, DFF])
    gate_d = dscr("gate_scr", [2, 5, D])

    banks = []
    for i in range(8):
        t = nc.alloc_psum_tensor(f"ps{i}", [128, 512], F32)
        banks.append(Tile(t.ap(), P.region(f"ps{i}", excl=True)))
    bank_rr = [0]

    def next_bank():
        k = bank_rr[0]
        bank_rr[0] = (k + 1) % 8
        return banks[k]

    cst = A.alloc("cst", C_TOT)
    cbf = A.alloc("cbf", 512 + 64 + 256, BF16)
    flg = A.alloc("flg", 2)
    modv = A.alloc("modv", 4 * KC * 5)
    scT = A.alloc("scT", KC * 5, BF16)
    b1p = A.alloc("b1p", 32)
    colT = A.alloc("colT", 128)
    cwT = A.alloc("cwT", 176)
    bias_all = A.alloc("bias_all", NH)
    bias_pre = A.alloc("bias_pre", NH)
    bias_ht = A.alloc("bias_ht", 64)
    qg_b = A.alloc("qg_b", HD)
    kg_b = A.alloc("kg_b", HD)
    gon_b = A.alloc("gon_b", 256)
    wgk = A.alloc("wgk", 512, BF16, parts=32)

    ident_f = cst[:, C_ID:C_ID + 128]
    ident_b = cbf[:, 0:128]
    triS_b = cbf[:, 128:256]
    ones_b = cbf[:, 256:384]
    mdiag_b = cbf[:, 384:512]
    m65_b = cbf[:, 512:576]
    tri8_b = cbf[:, 576:704]
    one8_b = cbf[:, 704:832]
    mdiag_f = cst[:, C_MDIAG:C_MDIAG + 128]
    m65_f = cst[:, C_M65:C_M65 + 64]
    modv4 = modv.re("p (v k b) -> p v k b", v=4, k=KC)

    b.dma("sp", cst, cst_d)
    b.dma("sp", flg, flg_d)
    b.copy(cbf[:, 0:512], cst[:, 0:512], "dve")
    b.copy(m65_b, m65_f, "dve")
    b.copy(cbf[:, 576:832], cst[:, C_TRI8:C_TRI8 + 256], "dve")
    b.dma("sp", bias_all, Tile(sbb_d.ap[0, :].partition_broadcast(128), None))
    b.dma("sp", qg_b, Tile(qng_d.ap[0, :].partition_broadcast(128), None))
    b.dma("sp", kg_b, Tile(kng_d.ap[0, :].partition_broadcast(128), None))
    b.dma("sp", gon_b, Tile(gon_d.ap[0, :].partition_broadcast(128), None))
    b.ts(bias_pre, bias_all, flg[:, 0:1], op0=ALU.add)
    b.copy(bias_ht.re("p (h t) -> p h t", t=4), bias_all.un(2).bc([128, NH, 4]), "dve")

    A.mark()
    rows = A.alloc("rows", 128)
    b.dma("sp", rows[0:96, :], bmod_d)
    b.dma("sp", rows[96:112, :], n1g_d)
    b.dma("sp", rows[112:128, :], n2g_d)
    bk = next_bank()
    b.tr(bk[:, 0:128], rows, ident_f)
    b.copy(colT, bk[:, 0:128], "dve")
    rowsA = A.alloc("rowsA", 128)
    rowsB = A.alloc("rowsB", 128)
    b.dma("sp", rowsA, Tile(cw_d.ap[0:128, :], None))
    b.dma("sp", rowsB[0:4, :], Tile(cw_d.ap[128:132, :], None))
    b.dma("sp", rowsB[4:48, :], cb_d)
    bk = next_bank()
    b.tr(bk[:, 0:128], rowsA, ident_f)
    b.tr(bk[:, 128:176], rowsB[0:48, :], ident_f[0:48, 0:48])
    b.copy(cwT, bk[:, 0:176], "dve")
    wgk_f = A.alloc("wgk_f", 512, F32, parts=32)
    b.dma("sp", wgk_f[0:16, :], wgk_d)
    b.dma("sp", wgk_f[16:17, :], bgk_d)
    b.copy(wgk[0:17, :], wgk_f[0:17, :], "dve")
    c_sb = A.alloc("c_sb", KC * 5)
    b.dma("sp", c_sb.re("p (k b) -> p k b", k=KC), Tile(ccT_d.ap.rearrange("(k p) b -> p k b", p=128), None))
    b.act(scT, c_sb, AF.Silu)
    scT3 = scT.re("p (k b) -> p k b", k=KC)
    b.ts(b1p[:, 0:16], colT[:, 16:32], 1.0, op0=ALU.add)
    b.ts(b1p[:, 16:32], colT[:, 64:80], 1.0, op0=ALU.add)
    def mod_load(j, c0, ncols, W, brow_t):
        W3 = W[:, 0:KC * ncols].re("p (k n) -> p k n", k=KC)
        b.dma("pool", W3, Tile(wmod_d.ap[:, j * D + c0:j * D + c0 + ncols].rearrange("(k p) n -> p k n", p=128), None))
        if j in (2, 5):
            b.dma("sp", brow_t[:, 0:ncols], Tile(bmodr_d.ap[0, j * D + c0:j * D + c0 + ncols].partition_broadcast(5), None))

    def mod_block(j, c0, ncols, W, brow_t, gst_t, bk, load=True):
        W3 = W[:, 0:KC * ncols].re("p (k n) -> p k n", k=KC)
        if load:
            mod_load(j, c0, ncols, W, brow_t)
        if j in (0, 1, 3, 4):
            vi = {1: 0, 0: 1, 4: 2, 3: 3}[j]
            for fc in range(ncols // 128):
                kcf = c0 // 128 + fc
                col = bk[:, fc * 8:fc * 8 + 5]
                for kc in range(KC):
                    b.mm(col, W3[:, kc, fc * 128:(fc + 1) * 128], scT3[:, kc, :],
                         start=(kc == 0), stop=(kc == KC - 1))
                if j == 1:
                    b.ts(modv4[:, vi, kcf, :], col, b1p[:, kcf:kcf + 1], colT[:, 96 + kcf:97 + kcf],
                         op0=ALU.add, op1=ALU.mult)
                elif j == 4:
                    b.ts(modv4[:, vi, kcf, :], col, b1p[:, 16 + kcf:17 + kcf],
                         colT[:, 112 + kcf:113 + kcf], op0=ALU.add, op1=ALU.mult)
                else:
                    b.ts(modv4[:, vi, kcf, :], col, colT[:, j * 16 + kcf:j * 16 + kcf + 1], op0=ALU.add)
        else:
            gi = 0 if j == 2 else 1
            for kc in range(KC):
                b.mm(bk[0:5, 0:ncols], scT3[:, kc, :], W3[:, kc, :], start=(kc == 0), stop=(kc == KC - 1))
            b.tt(gst_t[:, 0:ncols], bk[0:5, 0:ncols], brow_t[:, 0:ncols], ALU.add)
            b.dma("sp", Tile(gate_d.ap[gi, :, c0:c0 + ncols], gate_d.reg), gst_t[:, 0:ncols])

    wmb = [A.alloc(f"wmb{i}", KC * 512, BF16) for i in range(2)]
    blk_i = 0
    for j in (1, 0):
        for q in range(4):
            mod_block(j, q * 512, 512, wmb[blk_i % 2], None, None, next_bank())
            blk_i += 1
    deferred_mod = [(j, q * 256) for j in (4, 3, 2, 5) for q in range(8)]
    A.release()
    S1, SH1, S2, SH2 = 0, 1, 2, 3

    wk = {}
    SGROUPS = [(0, 4, 1), (4, 8, 2), (8, 12, 3), (12, 16, 4)]
    PGROUP = [(0, 128, 0)]

    def norm_stats(xt, n, pfx, slot):
        junk = wk[pfx + "junk"]
        st = wk[pfx + "st"][slot % 2]
        xn = wk[pfx + "xn"][slot % 2]
        b.memset(st[:n, 0:1], 0.0, "dve")
        b.act(junk[:n, :], xt[:n, :], AF.Square, accum=st[:n, 0:1])
        b.act(st[:n, 1:2], st[:n, 0:1], AF.Ln, scale=1.0 / D, bias=EPS)
        b.act(st[:n, 2:3], st[:n, 1:2], AF.Exp, scale=-0.5)
        b.ts(xn[:n, :], xt[:n, :], st[:n, 2:3], op0=ALU.mult)

    def norm_transpose(n, si, shi, outT, col0, groups, pfx, slot, bankpair=None):
        xn = wk[pfx + "xn"][slot % 2]
        bkA, bkB = bankpair if bankpair is not None else (next_bank(), next_bank())
        for kc in range(KC):
            bk = bkA if kc < 8 else bkB
            pb = bk.bitcast(BF16)
            k8 = kc % 8
            b.tr(pb[:, k8 * 128:k8 * 128 + n], xn[:n, kc * 128:(kc + 1) * 128], ident_b[:n, :n])
        ei = 0
        for kc in range(KC):
            bk = bkA if kc < 8 else bkB
            pb = bk.bitcast(BF16)
            k8 = kc % 8
            for (lo, hi, bi) in groups:
                src = pb[:, k8 * 128 + lo:k8 * 128 + hi]
                dst = outT[:, kc, col0 + lo:col0 + hi]
                sc_ = modv4[:, si, kc, bi:bi + 1]
                sh_ = modv4[:, shi, kc, bi:bi + 1]
                if ei % 2 == 0:
                    b.act(dst, src, AF.Identity, bias=sh_, scale=sc_)
                else:
                    b.ts(dst, src, sc_, sh_, op0=ALU.mult, op1=ALU.add)
                ei += 1

    def norm_pipeline(tiles, load_fn, si, shi, outT, pfx, xbuf):
        nt_ = len(tiles)
        assert len(xbuf) >= 3

        def ld(t):
            load_fn(t, xbuf[t % len(xbuf)])

        def s1(t):
            norm_stats(xbuf[t % len(xbuf)], tiles[t][0], pfx, t)
        ld(0)
        if nt_ > 1:
            ld(1)
        s1(0)
        for t in range(nt_):
            if t + 2 < nt_:
                ld(t + 2)
            if t + 1 < nt_:
                s1(t + 1)
            n, col0, groups = tiles[t]
            norm_transpose(n, si, shi, outT, col0, groups, pfx, t)

    def load_w_block(W, src_d, c0, ncols, nk=KC, r0=0):
        W3 = W[:, 0:nk * ncols].re("p (k n) -> p k n", k=nk)
        b.dma("pool", W3, Tile(src_d.ap[r0:r0 + nk * 128, c0:c0 + ncols].rearrange("(k p) n -> p k n", p=128), None))
        return W3

    def alloc_norm_work(pfx, rngs=(None, None, None)):
        wk[pfx + "junk"] = A.alloc(pfx + "junk", D, BF16, rng=rngs[0])
        wk[pfx + "st"] = [A.alloc(pfx + f"st{i}", 4) for i in range(2)]
        wk[pfx + "xn"] = [A.alloc(pfx + f"xn{i}", D, BF16, rng=rngs[1 + i]) for i in range(2)]

    def alloc_qk_work(pfx):
        wk[pfx + "sq"] = A.alloc(pfx + "sq", 512)
        wk[pfx + "st8"] = A.alloc(pfx + "st8", 24)
        wk[pfx + "tmp"] = A.alloc(pfx + "tmp", 512)

    def qk_post(ps, n, gtile, out_f32, out_bf, pfx):
        sq = wk[pfx + "sq"]
        st8 = wk[pfx + "st8"]
        tmp = wk[pfx + "tmp"]
        b.act(sq[:n, :], ps[:n, :], AF.Square)
        b.reduce(st8[:n, 0:8], sq[:n, :].re("p (h d) -> p h d", h=8))
        b.act(st8[:n, 8:16], st8[:n, 0:8], AF.Ln, scale=1.0 / HD, bias=EPS)
        b.act(st8[:n, 16:24], st8[:n, 8:16], AF.Exp, scale=-0.5)
        b.tt(tmp[:n, :].re("p (h d) -> p h d", h=8), ps[:n, :].re("p (h d) -> p h d", h=8),
             st8[:n, 16:24].un(2).bc([n, 8, HD]), ALU.mult)
        gb = gtile[:n, :].un(1).bc([n, 8, HD])
        if out_f32 is not None:
            b.tt(out_f32.re("p (h d) -> p h d", h=8), tmp[:n, :].re("p (h d) -> p h d", h=8), gb, ALU.mult)
            b.copy(out_bf, out_f32, "pool")
        else:
            b.tt(out_bf.re("p (h d) -> p h d", h=8), tmp[:n, :].re("p (h d) -> p h d", h=8), gb, ALU.mult)

    def transpose_to(dst3, col0, src_bf, n, nchunks, c_off, eng, bank=None):
        bk = bank if bank is not None else next_bank()
        pb = bk.bitcast(BF16)
        for c in range(nchunks):
            b.tr(pb[:, c * 128:c * 128 + n], src_bf[:n, c * 128:(c + 1) * 128], ident_b[:n, :n])
        src = pb[:, 0:nchunks * 128].re("p (c n) -> p c n", c=nchunks)[:, :, 0:n]
        b.copy(dst3[:, c_off:c_off + nchunks, col0:col0 + n], src, eng)

    def proj_tm(hT3, tiles, src_d, c0, ncols, consume, wbufs, blk):
        nb = (ncols + blk - 1) // blk
        pend = []
        Wn = load_w_block(wbufs[0], src_d, c0, min(blk, ncols))
        for bi in range(nb):
            cw = min(blk, ncols - bi * blk)
            W3 = Wn
            if bi + 1 < nb:
                Wn = load_w_block(wbufs[(bi + 1) % len(wbufs)], src_d, c0 + (bi + 1) * blk,
                                  min(blk, ncols - (bi + 1) * blk))
            for (col0, n, tag) in tiles:
                bk = next_bank()
                for kc in range(KC):
                    b.mm(bk[:n, 0:cw], hT3[:, kc, col0:col0 + n], W3[:, kc, :], start=(kc == 0), stop=(kc == KC - 1))
                pend.append((bi * blk, cw, tag, n, bk))
                if len(pend) > 2:
                    consume(*pend.pop(0))
        while pend:
            consume(*pend.pop(0))

    def proj_fm(hT3, ntok, src_d, c0, ncols, consume, wbufs, blk):
        nb = (ncols + blk - 1) // blk
        Wn = load_w_block(wbufs[0], src_d, c0, min(blk, ncols))
        for bi in range(nb):
            cwb = min(blk, ncols - bi * blk)
            W3 = Wn
            if bi + 1 < nb:
                Wn = load_w_block(wbufs[(bi + 1) % len(wbufs)], src_d, c0 + (bi + 1) * blk,
                                  min(blk, ncols - (bi + 1) * blk))
            for ci in range((cwb + 127) // 128):
                cw = min(128, cwb - ci * 128)
                for t0 in range(0, ntok, 512):
                    tw = min(512, ntok - t0)
                    bk = next_bank()
                    for kc in range(KC):
                        b.mm(bk[:cw, 0:tw], W3[:, kc, ci * 128:ci * 128 + cw], hT3[:, kc, t0:t0 + tw],
                             start=(kc == 0), stop=(kc == KC - 1))
                    consume((bi * blk) // 128 + ci, t0, tw, cw, bk)

    def gla_pass(pfx, hT3, ntok, tiles, want_out, states, wbufs, blk, oT3=None):
        gdT = A.alloc(pfx + "gdT", ntok, BF16, parts=32)
        b.memset(gdT, 1.0, "pool")
        ntl = len(tiles)
        sp_tok = A.alloc(pfx + "sp", ntl * 512)
        gk_tok = A.alloc(pfx + "gk", ntl * 512, BF16)
        gv_tok = A.alloc(pfx + "gv", ntl * 1024, BF16)
        sp3 = sp_tok.re("p (t n) -> p t n", t=ntl)
        gk3 = gk_tok.re("p (t n) -> p t n", t=ntl)
        gv3 = gv_tok.re("p (t n) -> p t n", t=ntl)
        if want_out:
            gqT = A.alloc(pfx + "gqT", 4 * ntok, BF16)
            gkT = A.alloc(pfx + "gkT", 4 * ntok, BF16)
            gqT3 = gqT.re("p (h n) -> p h n", h=4)
            gkT3 = gkT.re("p (h n) -> p h n", h=4)
            grs = A.alloc(pfx + "grs", ntl * 1024, BF16)
            grs3 = grs.re("p (t n) -> p t n", t=ntl)
        wsm = A.alloc(pfx + "wsm", KC * 16, BF16)
        ebuf = A.alloc(pfx + "ebuf", 512)
        W3 = load_w_block(wsm, win_d, OFF_GD, 16)
        for t0 in range(0, ntok, 512):
            tw = min(512, ntok - t0)
            bk = next_bank()
            for kc in range(KC):
                b.mm(bk[:16, 0:tw], W3[:, kc, :], hT3[:, kc, t0:t0 + tw], start=(kc == 0), stop=(kc == KC - 1))
            b.copy(gdT[0:16, t0:t0 + tw], bk[:16, 0:tw], "act")
        for li, tl in enumerate(tiles):
            n, c0 = tl["n"], tl["col0"]
            bk = next_bank()
            b.mm(bk[:n, :], gdT[0:17, c0:c0 + n], wgk[0:17, :])
            b.act(ebuf[:n, :], bk[:n, :], AF.Exp, scale=-1.0)
            b.act(sp3[:n, li, :], ebuf[:n, :], AF.Ln, scale=1.0, bias=1.0)
        tm_tiles = [(tl["col0"], tl["n"], li) for li, tl in enumerate(tiles)]

        def cons_gk(cofs, cw, li, n, bk):
            b.copy(gk3[:n, li, cofs:cofs + cw], bk[:n, 0:cw], "act")

        def cons_gv(cofs, cw, li, n, bk):
            b.copy(gv3[:n, li, cofs:cofs + cw], bk[:n, 0:cw], "act" if (cofs // blk) % 2 else "dve")

        proj_tm(hT3, tm_tiles, win_d, OFF_GK, 512, cons_gk, wbufs, blk)
        proj_tm(hT3, tm_tiles, win_d, OFF_GV, 1024, cons_gv, wbufs, blk)
        if want_out:
            def cons_gq(ci, t0, tw, cw, bk):
                b.act(gqT3[:, ci, t0:t0 + tw], bk[:, 0:tw], AF.Copy, scale=128.0 ** -0.5)

            def cons_gkT(ci, t0, tw, cw, bk):
                b.copy(gkT3[:, ci, t0:t0 + tw], bk[:, 0:tw], "dve")

            def cons_gr(cofs, cw, li, n, bk):
                b.act(grs3[:n, li, cofs:cofs + cw], bk[:n, 0:cw], AF.Silu)

            proj_fm(hT3, ntok, win_d, OFF_GQ, 512, cons_gq, wbufs, blk)
            proj_fm(hT3, ntok, win_d, OFF_GK, 512, cons_gkT, wbufs, blk)
            proj_tm(hT3, tm_tiles, win_d, OFF_GR, 1024, cons_gr, wbufs, blk)
        eb = A.alloc(pfx + "EbT", 512)
        el = A.alloc(pfx + "EL", 512)
        kp_ = A.alloc(pfx + "kp", 512, BF16)
        if want_out:
            en = A.alloc(pfx + "EnT", 512)
            qt = A.alloc(pfx + "qtT", 512, BF16)
            kt = A.alloc(pfx + "ktT", 512, BF16)
            pm = A.alloc(pfx + "Pm", 512, BF16)
            qseg = A.alloc(pfx + "qseg", 4 * 64, BF16)
            on = A.alloc(pfx + "on", 1024)
            onb = A.alloc(pfx + "onb", 1024, BF16)
            st4 = A.alloc(pfx + "st4", 12)
        kseg = A.alloc(pfx + "kseg", 512, BF16, parts=16)
        bkT, bkL, bkS = banks[0], banks[1], banks[2]
        bo = [banks[3], banks[4]]
        bs = [banks[5], banks[6]]
        for li, tl in enumerate(tiles):
            n, c0 = tl["n"], tl["col0"]
            minc, mstl, mc = tl["masks"]
            segs = tl["segs"]
            for hd in range(4):
                b.mm(bkT[:, hd * n:(hd + 1) * n], sp3[:n, li, hd * 128:(hd + 1) * 128], minc)
            b.mm(bkL[:n, :], mstl, sp3[:n, li, :])
            b.act(eb[:, 0:4 * n], bkT[:, 0:4 * n], AF.Exp)
            b.act(el[:n, :], bkL[:n, :], AF.Exp)
            b.tt(kp_[:n, :], gk3[:n, li, :], el[:n, :], ALU.mult)
            if want_out:
                b.act(en[:, 0:4 * n], bkT[:, 0:4 * n], AF.Exp, scale=-1.0)
                qt3 = qt[:, 0:4 * n].re("p (h n) -> p h n", h=4)
                kt3 = kt[:, 0:4 * n].re("p (h n) -> p h n", h=4)
                b.tt(qt3, gqT3[:, :, c0:c0 + n], eb[:, 0:4 * n].re("p (h n) -> p h n", h=4), ALU.mult)
                b.tt(kt3, gkT3[:, :, c0:c0 + n], en[:, 0:4 * n].re("p (h n) -> p h n", h=4), ALU.mult)
                for hd in range(4):
                    b.mm(bkS[:n, hd * n:(hd + 1) * n], kt3[:, hd, :], qt3[:, hd, :])
                pm3 = pm[:n, 0:4 * n].re("p (h n) -> p h n", h=4)
                b.tt(pm3, bkS[:n, 0:4 * n].re("p (h n) -> p h n", h=4), mc.un(1).bc([n, 4, n]), ALU.mult)
                if len(segs) > 1:
                    for si in range(len(segs)):
                        for hd in range(4):
                            b.tt(qseg[:, si * 64 + hd * 16:si * 64 + hd * 16 + 16], qt3[:, hd, :],
                                 cst[:, C_SEGC + si * 16:C_SEGC + si * 16 + 16], ALU.mult)
                for hd in range(4):
                    ob = bo[hd // 2][:n, (hd % 2) * 256:(hd % 2 + 1) * 256]
                    b.mm(ob, pm3[:, hd, :], gv3[:n, li, hd * 256:(hd + 1) * 256], start=True, stop=False)
                    for si, (lo, hi, sidx) in enumerate(segs):
                        if len(segs) == 1:
                            lq = qt3[:, hd, :]
                        else:
                            lq = qseg[:, si * 64 + hd * 16:si * 64 + hd * 16 + 16]
                        b.mm(ob, lq, states[sidx][1][:, hd * 256:(hd + 1) * 256], start=False,
                             stop=(si == len(segs) - 1))
                for hf in range(2):
                    b.act(on[:n, hf * 512:(hf + 1) * 512], bo[hf][:n, :], AF.Square)
                b.reduce(st4[:n, 0:4], on[:n, :].re("p (h d) -> p h d", h=4))
                b.act(st4[:n, 4:8], st4[:n, 0:4], AF.Ln, scale=1.0 / 256.0, bias=EPS)
                b.act(st4[:n, 8:12], st4[:n, 4:8], AF.Exp, scale=-0.5)
                for hf in range(2):
                    b.tt(on[:n, hf * 512:(hf + 1) * 512].re("p (h d) -> p h d", h=2),
                         bo[hf][:n, :].re("p (h d) -> p h d", h=2),
                         st4[:n, 8 + 2 * hf:10 + 2 * hf].un(2).bc([n, 2, 256]), ALU.mult)
                b.tt(on[:n, :].re("p (h d) -> p h d", h=4), on[:n, :].re("p (h d) -> p h d", h=4),
                     gon_b[:n, :].un(1).bc([n, 4, 256]), ALU.mult)
                b.tt(onb[:n, :], on[:n, :], grs3[:n, li, :], ALU.mult)
                transpose_to(oT3, c0, onb, n, 8, 0, "act", bank=banks[7])
            for si, (lo, hi, sidx) in enumerate(segs):
                S, Sb = states[sidx]
                if len(segs) == 1:
                    kps = kp_
                else:
                    kps = kseg
                    b.ts(kseg[:n, :], kp_[:n, :], cst[:n, C_SEGR + si:C_SEGR + si + 1], op0=ALU.mult)
                for hd in range(4):
                    sb_ = bs[hd // 2][:, (hd % 2) * 256:(hd % 2 + 1) * 256]
                    b.mm(sb_, kps[:n, hd * 128:(hd + 1) * 128], gv3[:n, li, hd * 256:(hd + 1) * 256])
                for hd in range(4):
                    sb_ = bs[hd // 2][:, (hd % 2) * 256:(hd % 2 + 1) * 256]
                    ecol = eb[:, hd * n + hi - 1:hd * n + hi]
                    b.stt(S[:, hd * 256:(hd + 1) * 256], S[:, hd * 256:(hd + 1) * 256], ecol, sb_, ALU.mult, ALU.add)
                b.copy(Sb, S, "pool")

    masks128 = (cst[:, C_MINC:C_MINC + 128], cst[:, C_MSTL:C_MSTL + 128], cst[:, C_MC:C_MC + 128])
    masks16 = (cst[0:16, C_MINC16:C_MINC16 + 16], cst[0:16, C_MSTL16:C_MSTL16 + 16], cst[0:16, C_MC16:C_MC16 + 16])

    CAP = A.cap
    PBASE = (A.tops[0] + 63) // 64 * 64
    o_hT = PBASE
    o_kT = o_hT + KC * TT * 2
    o_vtok = o_kT + 8 * 2048 * 2
    end_kv = o_vtok + 16 * 1024 * 2
    o_qT = end_kv
    o_kTs = o_qT + 8 * TT * 2
    o_vs = o_kTs + 8 * TS * 2
    end_q = o_vs + 1024 * 2
    o_oTsb = end_q
    end_o = o_oTsb + 8 * TT * 2
    o_Sp = CAP - 6144
    o_oTgl = o_kT
    o_mixT = o_oTgl + 8 * TT * 2
    end_mix = o_mixT + KC * TT * 2
    o_gT = CAP - (NFF * TT * 2 + 1792 + 1408)
    assert end_o < o_Sp and end_mix <= o_oTsb and o_gT > o_kT

    A.set_ranges([(end_kv, o_Sp)])
    Sp = A.fixed("Sp", 1024, F32, o_Sp)
    Spb = A.fixed("Spb", 1024, BF16, o_Sp + 4096)
    kT = A.fixed("kT", 8 * 2048, BF16, o_kT)
    kT3 = kT.re("p (c n) -> p c n", c=8)
    vtok = A.fixed("vtok", 16 * 1024, BF16, o_vtok)
    vtok3 = vtok.re("p (t n) -> p t n", t=16)
    b.memset(Sp, 0.0, "pool")
    b.memset(Spb, 0.0, "pool")
    hTp = A.fixed("hTp", KC * TPRE, BF16, o_hT)
    hTp3 = hTp.re("p (k n) -> p k n", k=KC)
    A.mark()
    alloc_norm_work("a_")
    xbuf = [A.alloc(f"xbuf{i}", D) for i in range(3)]
    norm_pipeline([(128, ti * 128, PGROUP) for ti in range(NP_)],
                  lambda t, xt: b.dma("sp", xt, Tile(xp_d.ap[t * 128:(t + 1) * 128, :], None)),
                  S1, SH1, hTp3, "a_", xbuf)
    A.release()
    wb = [A.alloc(f"wb{i}", KC * 512, BF16) for i in range(2)]
    A.mark()
    alloc_qk_work("a_")
    kbf = [A.alloc(f"kbf{i}", 512, BF16) for i in range(3)]
    cnt = [0]
    pre_tiles = [(ti * 128, 128, ti) for ti in range(NP_)]

    def cons_kv_pre(cofs, cw, ti, n, bk):
        i = cnt[0]
        cnt[0] += 1
        if cofs < 1024:
            kb_ = kbf[i % 3]
            qk_post(bk, n, kg_b, None, kb_[:n, :], "a_")
            transpose_to(kT3, ti * 128, kb_, n, 4, (cofs // 512) * 4, "act")
        else:
            b.copy(vtok3[:n, ti, cofs - 1024:cofs - 1024 + cw], bk[:n, 0:cw], "act" if i % 2 else "dve")

    proj_tm(hTp3, pre_tiles, win_d, OFF_K, 2048, cons_kv_pre, wb, 512)
    A.release()
    A.mark()
    gla_pass("gp_", hTp3, TPRE,
             [dict(col0=ti * 128, n=128, masks=masks128, segs=[(0, 128, 0)]) for ti in range(NP_)],
             False, [(Sp, Spb)], wb, 512)
    A.release()
    b.ts(Sp, Sp, flg[:, 1:2], op0=ALU.mult)
    b.copy(Spb, Sp, "pool")

    A.set_ranges([(end_o, o_Sp)])
    hT = A.fixed("hT", KC * TT, BF16, o_hT)
    hT3 = hT.re("p (k n) -> p k n", k=KC)
    qT = A.fixed("qT", 8 * TT, BF16, o_qT)
    qT3 = qT.re("p (c n) -> p c n", c=8)
    kTs = A.fixed("kTs", 8 * TS, BF16, o_kTs)
    kTs3 = kTs.re("p (c n) -> p c n", c=8)
    vs_tok = A.fixed("vs_tok", 1024, BF16, o_vs, parts=16)
    oTsb = A.fixed("oTsb", 8 * TT, BF16, o_oTsb)
    oTsb3 = oTsb.re("p (c n) -> p c n", c=8)
    A.mark()
    alloc_norm_work("b_")
    xbuf = [A.alloc(f"xbufb{i}", D) for i in range(3)]

    def load_main(t, xt):
        if t < NT:
            b.dma("sp", xt, Tile(xm_d.ap[t * 128:(t + 1) * 128, :], None))
        else:
            b.dma("sp", xt[:TS, :], xs_d)

    norm_pipeline([(128, ti * 128, PGROUP) for ti in range(NT)] + [(TS, T, SGROUPS)],
                  load_main, S1, SH1, hT3, "b_", xbuf)
    A.release()
    A.mark()
    alloc_qk_work("b_")
    kbf = [A.alloc(f"kbfb{i}", 512, BF16) for i in range(3)]
    kf32 = [A.alloc(f"kf32b{i}", 512) for i in range(4)]
    wb = [A.alloc(f"wbb{i}", KC * 512, BF16) for i in range(2)]
    main_tiles = [(ti * 128, 128, ti) for ti in range(NT)] + [(T, TS, NT)]
    cnt[0] = 0

    def cons_qkv(cofs, cw, ti, n, bk):
        i = cnt[0]
        cnt[0] += 1
        is_s = (ti == NT)
        if cofs < 1024:
            qb = kbf[i % 3]
            qk_post(bk, n, qg_b, None, qb[:n, :], "b_")
            transpose_to(qT3, ti * 128, qb, n, 4, (cofs // 512) * 4, "act")
        elif cofs < 2048:
            kb_ = kbf[i % 3]
            kf = kf32[i % 4]
            qk_post(bk, n, kg_b, kf[:n, :], kb_[:n, :], "b_")
            cs = cofs - 1024
            if is_s:
                b.dma("sp", Tile(ks_d.ap[:, cs:cs + 512], ks_d.reg), kf[:n, :], is_output=True)
                transpose_to(kTs3, 0, kb_, n, 4, (cs // 512) * 4, "act")
            else:
                b.dma("sp", Tile(kp_d.ap[ti * 128:(ti + 1) * 128, cs:cs + 512], kp_d.reg), kf[:n, :], is_output=True)
                transpose_to(kT3, TPRE + ti * 128, kb_, n, 4, (cs // 512) * 4, "act")
        else:
            kf = kf32[i % 4]
            cs = cofs - 2048
            b.copy(kf[:n, :], bk[:n, :], "act")
            if is_s:
                b.dma("sp", Tile(vs_d.ap[:, cs:cs + 512], vs_d.reg), kf[:n, :], is_output=True)
                b.copy(vs_tok[:n, cs:cs + 512], kf[:n, :], "pool")
            else:
                b.dma("sp", Tile(vp_d.ap[ti * 128:(ti + 1) * 128, cs:cs + 512], vp_d.reg), kf[:n, :], is_output=True)
                b.copy(vtok3[:n, NP_ + ti, cs:cs + 512], kf[:n, :], "pool")

    proj_tm(hT3, main_tiles, win_d, OFF_Q, 3072, cons_qkv, wb, 512)
    A.release()

    A.mark()
    NQ = 384
    e2b = [A.alloc(f"e2b{i}", NQ) for i in range(2)]
    spb_ = [A.alloc(f"spb{i}", NQ, BF16) for i in range(3)]
    ab_ = [A.alloc(f"ab{i}", NQ, BF16) for i in range(3)]
    L32 = [A.alloc(f"L32{i}", NQ) for i in range(4)]
    L16 = [A.alloc(f"L16{i}", NQ, BF16) for i in range(4)]
    otok = A.alloc("otok", 3 * 1024, BF16)
    otok3 = otok.re("p (q n) -> p q n", q=3)
    zbanks = [banks[0], banks[1]]
    abanks = [banks[2], banks[3]]
    obanks = [banks[4], banks[5]]
    tbanks = [banks[6], banks[7]]
    units = []
    pr = 0
    for g in range(3):
        for c_ in range(NH // 2):
            kmax = NP_ + 3 * g + 2
            for kb in range(kmax, -1, -1):
                for hh in range(2):
                    units.append(dict(g=g, h=2 * c_ + hh, hh=hh, kb=kb, kmax=kmax, gh=2 * pr + hh, pr=pr,
                                      u=len(units)))
            pr += 1

    def stage_a(U):
        g, h, kb, u = U["g"], U["h"], U["kb"], U["u"]
        c, po = h // 2, (h % 2) * 64
        i = kb - (NP_ + 3 * g)
        off = max(i, 0) * 128
        N = NQ - off
        qlo = NQ * g + off
        l32, l16 = L32[U["gh"] % 4], L16[U["gh"] % 4]
        if kb == U["kmax"]:
            b.memset(l32, 0.0, "pool")
            b.memset(l16, 0.0, "pool")
        zb = zbanks[u % 2]
        e2 = e2b[u % 2]
        spb = spb_[u % 3]
        bias_t = (bias_pre if kb < NP_ else bias_all)[:, h:h + 1]
        U.update(off=off, N=N, qlo=qlo, c=c, po=po, i=i, bias_t=bias_t, spb=spb, l32=l32, l16=l16)
        b.mm(zb[:, off:NQ], kT3[po:po + 64, c, kb * 128:(kb + 1) * 128], qT3[po:po + 64, c, qlo:qlo + N])
        b.act(e2[:, off:NQ], zb[:, off:NQ], AF.Exp, scale=0.125, bias=bias_t)
        b.act(spb[:, off:NQ], e2[:, off:NQ], AF.Ln, scale=1.0, bias=1.0)
        if i >= 0:
            b.tt(spb[:, off:off + 128], spb[:, off:off + 128], mdiag_b, ALU.mult, eng="pool")

    def stage_b(U):
        h, kb, u, off, N, qlo, c, po, i = (U[x] for x in ("h", "kb", "u", "off", "N", "qlo", "c", "po", "i"))
        spb, l32, l16, bias_t = U["spb"], U["l32"], U["l16"], U["bias_t"]
        afb = abanks[u % 2]
        ab = ab_[u % 3]
        U["ab"] = ab
        firstblk = (kb == U["kmax"])
        b.mm(afb[:, off:NQ], tri8_b, spb[:, off:NQ], start=True, stop=False)
        if not firstblk:
            b.mm(afb[:, off:NQ], one8_b, l16[:, off:NQ], start=False, stop=False)
        b.mm(afb[:, off:NQ], kT3[po:po + 64, c, kb * 128:(kb + 1) * 128], qT3[po:po + 64, c, qlo:qlo + N],
             start=False, stop=True)
        b.act(ab[:, off:NQ], afb[:, off:NQ], AF.Exp, scale=0.125, bias=bias_t)
        if i >= 0:
            b.tt(ab[:, off:off + 128], ab[:, off:off + 128], mdiag_b, ALU.mult, eng="pool")
        if kb > 0:
            b.tt(l32[:, off:NQ], l32[:, off:NQ], spb[:, off:NQ], ALU.add, eng="pool")
            b.copy(l16[:, off:NQ], l32[:, off:NQ], "dve")

    def stage_c(U):
        g, h, kb, i, hh = U["g"], U["h"], U["kb"], U["i"], U["hh"]
        ob = obanks[U["pr"] % 2]
        ab = U["ab"]
        for qi in range(max(i, 0), 3):
            first = (kb == U["kmax"] and qi == max(i, 0) and hh == 0)
            b.mm(ob[:, hh * 256 + qi * 64:hh * 256 + (qi + 1) * 64], ab[:, qi * 128:(qi + 1) * 128],
                 vtok3[:, kb, h * 64:(h + 1) * 64], start=first, stop=(kb == 0), skip=True)
        if kb == 0:
            b.copy(otok3[:, :, h * 64:(h + 1) * 64],
                   ob[:, hh * 256:hh * 256 + 192].re("p (q d) -> p q d", q=3), "dve")
            if h == NH - 1:
                for qi in range(3):
                    transpose_to(oTsb3, (3 * g + qi) * 128, otok3[:, qi, :], 128, 8, 0, "dve", bank=tbanks[qi % 2])

    dm_w = [A.alloc(f"dm_w{i}", KC * 256, BF16) for i in range(2)]
    dm_b = [A.alloc(f"dm_b{i}", 256, F32, parts=5) for i in range(2)]
    dm_g = [A.alloc(f"dm_g{i}", 256, F32, parts=5) for i in range(2)]
    nu = len(units)
    dmi = 0
    dml = 0
    for step in range(nu + 2):
        if step % 19 == 2 and dml < len(deferred_mod):
            j_, c0_ = deferred_mod[dml]
            mod_load(j_, c0_, 256, dm_w[dml % 2], dm_b[dml % 2])
            dml += 1
        if step % 19 == 17 and dmi < dml:
            j_, c0_ = deferred_mod[dmi]
            mod_block(j_, c0_, 256, dm_w[dmi % 2], dm_b[dmi % 2], dm_g[dmi % 2], tbanks[1], load=False)
            dmi += 1
        if step < nu:
            stage_a(units[step])
        if 0 <= step - 1 < nu:
            stage_b(units[step - 1])
        if 0 <= step - 2 < nu:
            stage_c(units[step - 2])
    assert dmi == len(deferred_mod)
    A.release()

    A.set_ranges([(o_kT, end_kv), (end_o, o_Sp)])
    A.mark()
    NB = NPAGES + 1
    zall = A.alloc("zall", NB * 64)
    zall3 = zall.re("p (j n) -> p j n", j=NB)
    sps = A.alloc("sps", NB * 64)
    sps3 = sps.re("p (j n) -> p j n", j=NB)
    Ls = A.alloc("Ls", NB * 64)
    Ls3 = Ls.re("p (j n) -> p j n", j=NB)
    kpg = [A.alloc(f"kpg{i}", 1024, BF16) for i in range(4)]
    kTp = [A.alloc(f"kTp{i}", 1024, BF16) for i in range(2)]
    knew = A.alloc("knew", 1024, BF16)
    knew3 = knew.re("p (c n) -> p c n", c=8)
    spsb = A.alloc("spsb", NB * 64, BF16)
    Lsb = A.alloc("Lsb", NB * 64, BF16)
    aall = [A.alloc(f"aall{i}", NB * 64, BF16) for i in range(2)]
    vpg = [A.alloc(f"vpg{i}", 1024, BF16) for i in range(4)]
    vnew = A.alloc("vnew", 1024, BF16)
    t2s = A.alloc("t2s", 512)
    bsuf_t = A.alloc("bsuf", 512)
    ptb = t2s[:, 0:4 * NPAGES].bitcast(I32)
    ptf = t2s[:, 4 * NPAGES:8 * NPAGES]
    idxa = A.alloc("idxa", 4 * NPAGES, I32)
    qbd = A.alloc("qbd", 64, BF16)
    qbd3 = qbd.re("p (c n) -> p c n", c=8)
    b.dma("sp", ptb, Tile(pt_d.ap[0, :].partition_broadcast(128), None))
    b.copy(ptf, ptb, "dve")
    b.ts(ptf, ptf, 128.0, cst[:, C_IOTA:C_IOTA + 1], op0=ALU.mult, op1=ALU.add)
    b.copy(idxa, ptf, "dve")
    b.memset(knew, 0.0, "pool")
    b.memset(vnew, 0.0, "pool")
    b.memset(qbd, 0.0, "pool")
    zbank_s = [banks[0], banks[1]]
    trbank = [banks[2], banks[3]]
    obank_s = banks[4]
    afbank = [banks[5], banks[6]]
    pgc = [0]

    def s_kpass(sq_i):
        scol = T + 4 * sq_i
        b.copy(qbd3[0:64, :, 0:4], qT3[0:64, :, scol:scol + 4], "dve")
        b.copy(qbd3[64:128, :, 4:8], qT3[64:128, :, scol:scol + 4], "dve")
        b.copy(knew3[:, :, 0:4], kTs3[:, :, 4 * sq_i:4 * sq_i + 4], "dve")

        def qk(j, kt3):
            zb = zbank_s[(j // 8) % 2]
            zc = (j % 8) * 64
            for c in range(8):
                b.mm(zb[:, zc + c * 8:zc + c * 8 + 8], kt3[:, c, :], qbd3[:, c, :])
            if j % 8 == 7 or j == NB - 1:
                j0 = (j // 8) * 8
                nbk = j - j0 + 1
                b.stt(zall3[:, j0:j0 + nbk, :], zb[:, 0:nbk * 64].re("p (j n) -> p j n", j=nbk), 0.125,
                      bias_ht.un(1).bc([128, nbk, 64]), ALU.mult, ALU.add)

        pend = None
        for j in range(NPAGES):
            pg = pgc[0]
            pgc[0] += 1
            kp_ = kpg[pg % 4]
            ktp = kTp[pg % 2]
            tbk = trbank[pg % 2]
            b.gather(kp_, ck_d, idxa[:, sq_i * NPAGES + j:sq_i * NPAGES + j + 1])
            pb = tbk.bitcast(BF16)
            for c in range(8):
                b.tr(pb[:, c * 128:(c + 1) * 128], kp_[:, c * 128:(c + 1) * 128], ident_b)
            b.copy(ktp, pb, "act" if pg % 2 else "dve")
            if pend is not None:
                qk(*pend)
            pend = (j, ktp.re("p (c n) -> p c n", c=8))
        qk(*pend)
        qk(NB - 1, knew3)

    def s_post(sq_i):
        aa = aall[sq_i % 2]
        aa3 = aa.re("p (j n) -> p j n", j=NB)
        b.act(Ls, zall, AF.Exp)
        b.act(sps, Ls, AF.Ln, scale=1.0, bias=1.0)
        b.tt(sps3[:, NB - 1, :], sps3[:, NB - 1, :], m65_f, ALU.mult)
        b.copy(spsb, sps, "act")
        sp4 = sps[:, 0:NPAGES * 64].re("p (bt r n) -> p bt r n", bt=8, r=8)
        L4 = Ls[:, 0:NPAGES * 64].re("p (bt r n) -> p bt r n", bt=8, r=8)
        b.memset(Ls3[:, NB - 1, :], 0.0, "dve")
        b.memset(L4[:, :, 7, :], 0.0, "dve")
        for r in range(6, -1, -1):
            b.tt(L4[:, :, r, :], L4[:, :, r + 1, :], sp4[:, :, r + 1, :], ALU.add)
        btot = t2s.re("p (bt n) -> p bt n", bt=8)
        b.tt(btot, L4[:, :, 0, :], sp4[:, :, 0, :], ALU.add)
        bsuf = bsuf_t.re("p (bt n) -> p bt n", bt=8)
        b.copy(bsuf[:, 7, :], sps3[:, NB - 1, :], "dve")
        for bt in range(6, -1, -1):
            b.tt(bsuf[:, bt, :], bsuf[:, bt + 1, :], btot[:, bt + 1, :], ALU.add)
        b.tt(L4, L4, bsuf.un(2).bc([128, 8, 8, 64]), ALU.add)
        b.copy(Lsb, Ls, "act")
        b.tt(zall, zall, sps, ALU.subtract)
        for ch in range(0, NB * 64, 512):
            cwd = min(512, NB * 64 - ch)
            afb = afbank[(ch // 512) % 2]
            b.mm(afb[:, 0:cwd], triS_b, spsb[:, ch:ch + cwd], start=True, stop=False)
            b.mm(afb[:, 0:cwd], ones_b, Lsb[:, ch:ch + cwd], start=False, stop=True)
            b.tt(t2s[:, 0:cwd], zall[:, ch:ch + cwd], afb[:, 0:cwd], ALU.subtract)
            b.act(aa[:, ch:ch + cwd], t2s[:, 0:cwd], AF.Exp)
        b.tt(aa3[:, NB - 1, :], aa3[:, NB - 1, :], m65_b, ALU.mult)

    def s_vpass(sq_i):
        scol = T + 4 * sq_i
        aa3 = aall[sq_i % 2].re("p (j n) -> p j n", j=NB)
        b.dma("sp", vnew[0:4, :], vs_tok[4 * sq_i:4 * sq_i + 4, :])
        first = True
        for j in range(NB):
            if j < NPAGES:
                vp_ = vpg[j % 4]
                b.gather(vp_, cv_d, idxa[:, sq_i * NPAGES + j:sq_i * NPAGES + j + 1])
            else:
                vp_ = vnew
            for c in range(8):
                b.mm(obank_s[:, c * 64:(c + 1) * 64], vp_[:, c * 128:(c + 1) * 128], aa3[:, j, :],
                     start=first, stop=(j == NB - 1), skip=True)
                first = False
        for c in range(8):
            for hh in range(2):
                hcol = c * 64 + (2 * c + hh) * 4
                b.copy(oTsb3[hh * 64:(hh + 1) * 64, c, scol:scol + 4],
                       obank_s[hh * 64:(hh + 1) * 64, hcol:hcol + 4], "act" if hh else "dve")

    s_kpass(0)
    s_post(0)
    for sq_i in range(1, 4):
        s_kpass(sq_i)
        s_vpass(sq_i - 1)
        s_post(sq_i)
    s_vpass(3)
    A.release()

    A.set_ranges([(o_oTgl + 8 * TT * 2, o_oTsb), (end_o, o_Sp)])
    oTgl = A.fixed("oTgl", 8 * TT, BF16, o_oTgl)
    oTgl3 = oTgl.re("p (c n) -> p c n", c=8)
    A.mark()
    wb = [A.alloc(f"wbg{i}", KC * 256, BF16) for i in range(2)]
    for (t_lo, t_hi, with_s) in ((0, 3, False), (3, 6, False), (6, NT, True)):
        A.mark()
        states = [(Sp, Spb)]
        if with_s:
            Ss = []
            for i in range(4):
                S_ = A.alloc(f"Ss{i}", 1024)
                Sb_ = A.alloc(f"Ssb{i}", 1024, BF16)
                b.dma("sp", S_.re("p (h v) -> p h v", h=4), Tile(sg_d.ap[i].rearrange("h k v -> k h v"), None))
                b.copy(Sb_, S_, "pool")
                Ss.append((S_, Sb_))
            states = states + Ss
        cbase = t_lo * 128
        ntok = (t_hi - t_lo) * 128 + (TS if with_s else 0)
        hv = Tile(hT3.ap[:, :, cbase:cbase + ntok], hT3.reg)
        ov = Tile(oTgl3.ap[:, :, cbase:cbase + ntok], oTgl3.reg)
        tl = [dict(col0=(ti - t_lo) * 128, n=128, masks=masks128, segs=[(0, 128, 0)]) for ti in range(t_lo, t_hi)]
        if with_s:
            tl.append(dict(col0=(t_hi - t_lo) * 128, n=TS, masks=masks16,
                           segs=[(4 * i, 4 * i + 4, 1 + i) for i in range(4)]))
        gla_pass(f"gm{t_lo}_", hv, ntok, tl, True, states, wb, 256, oT3=ov)
        if with_s:
            b.dma("sp", Tile(sgp_d.ap.rearrange("h k v -> k h v"), sgp_d.reg), Sp.re("p (h v) -> p h v", h=4),
                  is_output=True)
            for i in range(4):
                b.dma("sp", Tile(sgs_d.ap[i].rearrange("h k v -> k h v"), sgs_d.reg),
                      Ss[i][0].re("p (h v) -> p h v", h=4), is_output=True)
        A.release()
    A.release()

    A.set_ranges([(end_mix, o_oTsb), (end_o, o_Sp)])
    mixT = A.fixed("mixT", KC * TT, BF16, o_mixT)
    mixT3 = mixT.re("p (k n) -> p k n", k=KC)
    A.mark()
    wsb = [A.alloc(f"wsb{i}", 8 * 256, BF16) for i in range(2)]
    wgl = [A.alloc(f"wgl{i}", 8 * 256, BF16) for i in range(2)]
    wms = [A.alloc(f"wms{i}", KC * 256, BF16) for i in range(2)]
    wmg = [A.alloc(f"wmg{i}", KC * 256, BF16) for i in range(2)]
    s1b = [A.alloc(f"s1b{i}", 512) for i in range(2)]
    s2b = [A.alloc(f"s2b{i}", 512) for i in range(2)]
    tbl = [(0, 512), (512, 512), (1024, TT - 1024)]
    it = 0
    for fg in range(8):
        Wsb = load_w_block(wsb[fg % 2], wbs_d, fg * 256, 256, nk=8)
        Wgl = load_w_block(wgl[fg % 2], wbg_d, fg * 256, 256, nk=8)
        Wms = load_w_block(wms[fg % 2], win_d, OFF_MSB + fg * 256, 256)
        Wmg = load_w_block(wmg[fg % 2], win_d, OFF_MGLA + fg * 256, 256)
        for f2 in range(2):
            fc = fg * 2 + f2
            cs = slice(f2 * 128, (f2 + 1) * 128)
            for (t0, tw) in tbl:
                bks = [banks[(it % 2) * 4 + k] for k in range(4)]
                s1, s2 = s1b[it % 2], s2b[it % 2]
                it += 1
                for kc in range(8):
                    b.mm(bks[0][:, 0:tw], Wsb[:, kc, cs], oTsb3[:, kc, t0:t0 + tw], start=(kc == 0), stop=(kc == 7))
                for kc in range(8):
                    b.mm(bks[1][:, 0:tw], Wgl[:, kc, cs], oTgl3[:, kc, t0:t0 + tw], start=(kc == 0), stop=(kc == 7))
                for kc in range(KC):
                    b.mm(bks[2][:, 0:tw], Wms[:, kc, cs], hT3[:, kc, t0:t0 + tw], start=(kc == 0), stop=(kc == KC - 1))
                for kc in range(KC):
                    b.mm(bks[3][:, 0:tw], Wmg[:, kc, cs], hT3[:, kc, t0:t0 + tw], start=(kc == 0), stop=(kc == KC - 1))
                b.act(s1[:, 0:tw], bks[2][:, 0:tw], AF.Sigmoid)
                b.act(s2[:, 0:tw], bks[3][:, 0:tw], AF.Sigmoid)
                b.tt(s1[:, 0:tw], s1[:, 0:tw], bks[0][:, 0:tw], ALU.mult)
                b.tt(s2[:, 0:tw], s2[:, 0:tw], bks[1][:, 0:tw], ALU.mult)
                b.tt(mixT3[:, fc, t0:t0 + tw], s1[:, 0:tw], s2[:, 0:tw], ALU.add)
    A.release()
    h2T3 = hT3

    A.set_ranges([(end_mix, CAP), (o_kT, o_mixT)])
    A.mark()
    woq = []
    for q in range(4):
        wq_ = A.alloc(f"wo{q}", KC * 512, BF16, rng=0)
        wq3 = wq_.re("p (k n) -> p k n", k=KC)
        b.dma("pool", wq3, Tile(wout_d.ap[:, q * 512:(q + 1) * 512].rearrange("(k p) n -> p k n", p=128), None))
        woq.append(wq3)
    xbuf = [A.alloc(f"xbufc{i}", D, rng=0) for i in range(3)]
    Gp = A.alloc("Gp", D, rng=0)
    Gs = A.alloc("Gs", D, F32, parts=16, rng=1)

    def load_gates(gi, Gp, Gs):
        b.dma("sp", Gp, Tile(gate_d.ap[gi, 0, :].partition_broadcast(128), gate_d.reg))
        for i in range(4):
            b.dma("sp", Gs[4 * i:4 * i + 4, :], Tile(gate_d.ap[gi, 1 + i, :].partition_broadcast(4), gate_d.reg))

    load_gates(0, Gp, Gs)
    alloc_norm_work("c_", rngs=(1, 1, 0))
    tq2 = [A.alloc("tq0", 512, rng=0)] * 2
    ytile_regs = [P.region(f"ytile{i}") for i in range(NT + 1)]

    def wo_mm(ti):
        n = 128 if ti < NT else TS
        c0 = ti * 128
        xt = xbuf[ti % 3]
        if ti < NT:
            b.dma("sp", xt, Tile(xm_d.ap[c0:c0 + 128, :], None))
        else:
            b.dma("sp", xt[:TS, :], xs_d)
        bks = []
        for q in range(4):
            bk = banks[(ti % 2) * 4 + q]
            for kc in range(KC):
                b.mm(bk[:n, :], mixT3[:, kc, c0:c0 + n], woq[q][:, kc, :],
                     start=(kc == 0), stop=(kc == KC - 1))
            bks.append(bk)
        return bks

    def wo_post(ti, bks):
        n = 128 if ti < NT else TS
        c0 = ti * 128
        xt = xbuf[ti % 3]
        G = Gp if ti < NT else Gs
        for q in range(4):
            tq = tq2[q % 2]
            b.tt(tq[:n, :], bks[q][:n, :], G[:n, q * 512:(q + 1) * 512], ALU.mult)
            b.tt(xt[:n, q * 512:(q + 1) * 512], xt[:n, q * 512:(q + 1) * 512], tq[:n, :], ALU.add)
        if ti < NT:
            b.dma("sp", Tile(y_d.ap[c0:c0 + 128, :], ytile_regs[ti]), xt)
        else:
            b.dma("sp", Tile(ys_d.ap, ytile_regs[ti]), xt[:TS, :])
        norm_stats(xt, n, "c_", ti)

    def wo_tr(ti, bks):
        n = 128 if ti < NT else TS
        norm_transpose(n, S2, SH2, h2T3, ti * 128, PGROUP if ti < NT else SGROUPS, "c_", ti,
                       bankpair=(bks[0], bks[1]))

    bks_of = {0: wo_mm(0)}
    for ti in range(NT + 2):
        if ti - 1 >= 0:
            wo_tr(ti - 1, bks_of[ti - 1])
        if ti + 1 <= NT:
            bks_of[ti + 1] = wo_mm(ti + 1)
        if ti <= NT:
            wo_post(ti, bks_of[ti])
    A.release()

    A.set_ranges([(o_kT, o_gT)])
    gT = A.fixed("gT", NFF * TT, BF16, o_gT)
    gT3 = gT.re("p (j n) -> p j n", j=NFF)
    cvT = A.fixed("cvT", NFF * 10, F32, o_gT + NFF * TT * 2)
    cvT3 = cvT.re("p (j n) -> p j n", j=NFF)
    scT_ = A.fixed("scT_", NFF * 8, F32, o_gT + NFF * TT * 2 + 1792)
    scT3_ = scT_.re("p (j n) -> p j n", j=NFF)
    A.mark()
    scrow = A.alloc("scrow", DFF, F32, parts=8)
    b.dma("sp", scrow, sc_d)
    for j0 in range(0, NFF, 8):
        nj = min(8, NFF - j0)
        bk = next_bank()
        for jj in range(nj):
            b.tr(bk[:, jj * 8:jj * 8 + 8], scrow[0:8, (j0 + jj) * 128:(j0 + jj + 1) * 128], ident_f[0:8, 0:8])
        b.copy(scT3_[:, j0:j0 + nj, :], bk[:, 0:nj * 8].re("p (j n) -> p j n", j=nj), "dve")
    A.release()
    A.mark()
    wug = [A.alloc(f"wug{i}", KC * 256, BF16) for i in range(2)]
    wuv = [A.alloc(f"wuv{i}", KC * 256, BF16) for i in range(2)]
    abuf = [A.alloc(f"abuf{i}", 2 + T) for i in range(2)]
    asb = [A.alloc(f"asb{i}", 24) for i in range(2)]
    cbuf = [A.alloc(f"cbuf{i}", T) for i in range(2)]
    csb = [A.alloc(f"csb{i}", 16) for i in range(2)]
    for i in range(2):
        b.memset(abuf[i][:, 0:2], 0.0, "pool")
    Wg_n = load_w_block(wug[0], wup_d, 0, 256)
    Wv_n = load_w_block(wuv[0], wup_d, DFF, 256)
    for jg in range(NFF // 2):
        Wg, Wv = Wg_n, Wv_n
        if jg + 1 < NFF // 2:
            Wg_n = load_w_block(wug[(jg + 1) % 2], wup_d, (jg + 1) * 256, 256)
            Wv_n = load_w_block(wuv[(jg + 1) % 2], wup_d, DFF + (jg + 1) * 256, 256)
        for j2 in range(2):
            j = jg * 2 + j2
            cs = slice(j2 * 128, (j2 + 1) * 128)
            ab, asq, cb_, csq = abuf[j % 2], asb[j % 2], cbuf[j % 2], csb[j % 2]
            as3 = asq.re("p (s n) -> p s n", s=4)
            cs3 = csq.re("p (s n) -> p s n", s=4)
            w0 = cwT[:, j:j + 1]
            w1 = cwT[:, 44 + j:45 + j]
            w2 = cwT[:, 88 + j:89 + j]
            cbias = cwT[:, 132 + j:133 + j]
            vbk = []
            for (t0, tw) in tbl:
                bg = next_bank()
                for kc in range(KC):
                    b.mm(bg[:, 0:tw], Wg[:, kc, cs], h2T3[:, kc, t0:t0 + tw], start=(kc == 0), stop=(kc == KC - 1))
                if t0 < 1024:
                    b.copy(ab[:, 2 + t0:2 + t0 + tw], bg[:, 0:tw], "act")
                else:
                    b.copy(ab[:, 2 + 1024:2 + T], bg[:, 0:T - 1024], "act")
                    b.copy(as3[:, :, 0:2], scT3_[:, j, :].re("p (s n) -> p s n", s=4), "pool")
                    b.copy(as3[:, :, 2:6], bg[:, T - 1024:TT - 1024].re("p (s n) -> p s n", s=4), "act")
            for (t0, tw) in tbl:
                bv = next_bank()
                for kc in range(KC):
                    b.mm(bv[:, 0:tw], Wv[:, kc, cs], h2T3[:, kc, t0:t0 + tw], start=(kc == 0), stop=(kc == KC - 1))
                vbk.append(bv)
            b.ts(cb_, ab[:, 2:2 + T], w2, cbias, op0=ALU.mult, op1=ALU.add)
            b.stt(cb_, ab[:, 1:1 + T], w1, cb_, ALU.mult, ALU.add)
            b.stt(cb_, ab[:, 0:T], w0, cb_, ALU.mult, ALU.add)
            b.act(cb_, cb_, AF.Silu)
            b.ts(cs3, as3[:, :, 2:6], w2, cbias, op0=ALU.mult, op1=ALU.add)
            b.stt(cs3, as3[:, :, 1:5], w1, cs3, ALU.mult, ALU.add)
            b.stt(cs3, as3[:, :, 0:4], w0, cs3, ALU.mult, ALU.add)
            b.act(csq, csq, AF.Silu)
            b.copy(cvT3[:, j, 0:2], ab[:, T:T + 2], "pool")
            b.copy(cvT3[:, j, 2:10].re("p (s n) -> p s n", s=4), as3[:, :, 4:6], "pool")
            for bi_, (t0, tw) in enumerate(tbl):
                if t0 < 1024:
                    b.tt(gT3[:, j, t0:t0 + tw], cb_[:, t0:t0 + tw], vbk[bi_][:, 0:tw], ALU.mult)
                else:
                    b.tt(gT3[:, j, 1024:T], cb_[:, 1024:T], vbk[bi_][:, 0:T - 1024], ALU.mult)
                    b.tt(gT3[:, j, T:TT], csq, vbk[bi_][:, T - 1024:TT - 1024], ALU.mult)
    A.release()
    A.set_ranges([(PBASE, o_gT)])
    A.mark()
    cvrow = A.alloc("cvrow", DFF, F32, parts=16)
    for j0 in range(0, NFF, 4):
        bk = next_bank()
        for jj in range(4):
            b.tr(bk[0:10, jj * 128:(jj + 1) * 128], cvT3[:, j0 + jj, :], ident_f)
        b.copy(cvrow[0:10, j0 * 128:(j0 + 4) * 128], bk[0:10, :], "act")
    b.dma("sp", cvp_d, cvrow[0:2, :], is_output=True)
    b.dma("sp", cvs_d, cvrow[2:10, :], is_output=True)
    A.release()

    A.mark()
    Gp = A.alloc("Gp2", D)
    Gs = A.alloc("Gs2", D, F32, parts=16)
    load_gates(1, Gp, Gs)
    HJ = NFF // 2
    wd = [A.alloc(f"wd{i}", HJ * 512, BF16) for i in range(2)]
    xmb = [A.alloc(f"xmb{i}", 512) for i in range(4)]
    ob_ = [A.alloc(f"ob{i}", 512) for i in range(4)]
    xi = 0
    Wh_n = [load_w_block(wd[half], wdn_d, 0, 512, nk=HJ, r0=half * HJ * 128) for half in range(2)]
    for sl in range(4):
        c_lo = sl * 512
        Wh = list(Wh_n)
        for pi, tiles_ in enumerate((list(range(8)), [8, 9])):
            for half in range(2):
                for jj in range(HJ):
                    j = half * HJ + jj
                    for bi_, ti in enumerate(tiles_):
                        n = 128 if ti < NT else TS
                        c0 = ti * 128
                        b.mm(banks[bi_][:n, :], gT3[:, j, c0:c0 + n], Wh[half][:, jj, :],
                             start=(j == 0), stop=(j == NFF - 1))
                if pi == 1 and sl + 1 < 4:
                    Wh_n[half] = load_w_block(wd[half], wdn_d, c_lo + 512, 512, nk=HJ, r0=half * HJ * 128)
            for bi_, ti in enumerate(tiles_):
                n = 128 if ti < NT else TS
                c0 = ti * 128
                G = Gp if ti < NT else Gs
                xm_ = xmb[xi % 4]
                o_ = ob_[xi % 4]
                xi += 1
                if ti < NT:
                    src = Tile(y_d.ap[c0:c0 + 128, c_lo:c_lo + 512], ytile_regs[ti])
                else:
                    src = Tile(ys_d.ap[:, c_lo:c_lo + 512], ytile_regs[ti])
                b.dma("sp", xm_[:n, :], src)
                b.tt(o_[:n, :], banks[bi_][:n, :], G[:n, c_lo:c_lo + 512], ALU.mult)
                b.tt(o_[:n, :], o_[:n, :], xm_[:n, :], ALU.add)
                b.dma("sp", src, o_[:n, :], is_output=True)
    A.release()

    P.emit()
    return nc, P, A


_CACHE = {}


def kernel(x_prompt, x_sample, c_prompt, c_sample, cache_k_sb, cache_v_sb, page_table, state_gla, state_ffn_conv,
           norm1_g, norm2_g, w_mod, b_mod, w_in, q_norm_g, k_norm_g, sb_bias, w_gk_up, b_gk, gla_onorm_g,
           w_br_sb, w_br_gla, w_out, w_up, conv_w, conv_b, w_down):
    f = lambda a: np.ascontiguousarray(np.asarray(a, dtype=np.float32))
    if "nc" not in _CACHE:
        _CACHE["nc"] = build_program()[0]
    nc = _CACHE["nc"]
    x_prompt = f(x_prompt)
    x_sample = f(x_sample)
    ck = f(cache_k_sb)[0].reshape(NPOOL_ROWS, 1024)
    cv = f(cache_v_sb)[0].reshape(NPOOL_ROWS, 1024)
    cst = make_consts()
    shared = {
        "ck": ck, "cv": cv, "cst": cst,
        "norm1_g": f(norm1_g)[0].reshape(16, 128), "norm2_g": f(norm2_g)[0].reshape(16, 128),
        "w_mod": f(w_mod)[0], "b_mod": f(b_mod)[0].reshape(96, 128), "b_mod_row": f(b_mod)[0].reshape(1, 6 * D),
        "w_in": f(w_in)[0], "q_norm_g": f(q_norm_g)[0].reshape(1, HD), "k_norm_g": f(k_norm_g)[0].reshape(1, HD),
        "sb_bias": f(sb_bias)[0].reshape(1, NH), "w_gk_up": f(w_gk_up)[0], "b_gk": f(b_gk)[0].reshape(1, 512),
        "gla_onorm_g": f(gla_onorm_g)[0].reshape(1, 256), "w_br_sb": f(w_br_sb)[0], "w_br_gla": f(w_br_gla)[0],
        "w_out": f(w_out)[0], "w_up": f(w_up)[0], "conv_w": f(conv_w)[0].reshape(132, 128),
        "conv_b": f(conv_b)[0].reshape(44, 128), "w_down": f(w_down)[0],
    }
    pt = np.asarray(page_table).astype(np.int32)
    in_maps = []
    for core in range(8):
        bq, half = core // 2, core % 2
        lo = 0 if half == 0 else 2048 - T
        flg = np.zeros((128, 2), np.float32)
        flg[:, 0] = PENALTY if half == 0 else 0.0
        flg[:, 1] = 0.0 if half == 0 else 1.0
        sl = slice(4 * core, 4 * core + 4)
        ccT = np.concatenate([f(c_prompt)[bq:bq + 1], f(c_sample)[sl]], axis=0).T
        m = dict(shared)
        m.update({
            "xm": np.ascontiguousarray(x_prompt[bq, lo:lo + T]),
            "xp": np.ascontiguousarray(x_prompt[bq, 0:TPRE]),
            "xs": np.ascontiguousarray(x_sample[sl].reshape(TS, D)),
            "flg": flg,
            "ccT": np.ascontiguousarray(ccT),
            "pt": np.ascontiguousarray(pt[sl].reshape(1, 4 * NPAGES)),
            "sg": np.ascontiguousarray(f(state_gla)[0, sl]),
            "sc": np.ascontiguousarray(f(state_ffn_conv)[0, sl].reshape(8, DFF)),
        })
        in_maps.append(m)
    res = run_bass_kernel_spmd(nc, in_maps, core_ids=list(range(8)))
    R = res.results
    y_prompt = np.zeros((4, 2048, D), np.float32)
    k_prompt = np.zeros((1, 4, 2048, NH, HD), np.float32)
    v_prompt = np.zeros((1, 4, 2048, NH, HD), np.float32)
    gla_p = np.zeros((1, 4, 4, 128, 256), np.float32)
    conv_p = np.zeros((1, 4, 2, DFF), np.float32)
    y_sample = np.zeros((32, 4, D), np.float32)
    k_sample = np.zeros((1, 32, 4, NH, HD), np.float32)
    v_sample = np.zeros((1, 32, 4, NH, HD), np.float32)
    gla_s = np.zeros((1, 32, 4, 128, 256), np.float32)
    conv_s = np.zeros((1, 32, 2, DFF), np.float32)
    for core in range(8):
        bq, half = core // 2, core % 2
        r = R[core]
        if half == 0:
            rows, dst = slice(0, 1024), slice(0, 1024)
        else:
            rows, dst = slice(T - 1024, T), slice(1024, 2048)
            gla_p[0, bq] = r["sgp"]
            conv_p[0, bq] = r["cvp"]
        y_prompt[bq, dst] = r["y"][rows]
        k_prompt[0, bq, dst] = r["kp"][rows].reshape(1024, NH, HD)
        v_prompt[0, bq, dst] = r["vp"][rows].reshape(1024, NH, HD)
        sl = slice(4 * core, 4 * core + 4)
        y_sample[sl] = r["ys"].reshape(4, 4, D)
        k_sample[0, sl] = r["ks"].reshape(4, 4, NH, HD)
        v_sample[0, sl] = r["vs"].reshape(4, 4, NH, HD)
        gla_s[0, sl] = r["sgs"]
        conv_s[0, sl] = r["cvs"].reshape(4, 2, DFF)
    return (y_prompt, y_sample, k_prompt, v_prompt, gla_p, conv_p, k_sample, v_sample, gla_s, conv_s)
```
